# Optimizing a Trainium2 kernel written in Bass

```python
import jax, jax.numpy as jnp
from jax import lax
import numpy as np

D_MODEL = 1024
BATCH = 8
SEQ = 8192
DEPTH = 4
DEC_BATCH = 16
DEC_SEQ = 16
PAST_LEN = 1024

CHUNK = 64
N_A = DEPTH // 2
N_B = DEPTH - N_A
SSM_EXPAND = 2
D_INNER = SSM_EXPAND * D_MODEL
SSM_HEAD_DIM = 64
SSM_HEADS = D_INNER // SSM_HEAD_DIM
SSM_GROUPS = 8
SSM_HEADS_PER_GROUP = SSM_HEADS // SSM_GROUPS
D_STATE = 128
CONV_W = 4
CONV_DIM = D_INNER + 2 * SSM_GROUPS * D_STATE
D_IN_PROJ = D_INNER + CONV_DIM + SSM_HEADS
SSD_CHUNK = CHUNK
SB_HEADS = 4
SB_HEAD_DIM = 128
SB_WIDTH = SB_HEADS * SB_HEAD_DIM
SB_BLOCK = 128
D_FF = 4 * D_MODEL
PLE_DIM = 256
ALPHA = (2.0 * DEPTH) ** 0.25
BETA = (8.0 * DEPTH) ** -0.25
LN_EPS = 1e-5
RMS_EPS = 1e-5

kernel_name = 'yoco_mamba2_stickbreaking_stream_step'


def layer_norm(x, g, b):
    xf = x.astype(jnp.float32)
    mu = jnp.mean(xf, axis=-1, keepdims=True)
    xc = xf - mu
    var = jnp.mean(xc * xc, axis=-1, keepdims=True)
    return (xc * lax.rsqrt(var + LN_EPS) * g.astype(jnp.float32) + b.astype(jnp.float32)).astype(x.dtype)


def causal_dwconv(u, prev, w, b):
    T = u.shape[1]
    up = jnp.concatenate([prev.astype(u.dtype), u], axis=1)
    y = up[:, 0:T] * w[0]
    for kk in range(1, CONV_W):
        y = y + up[:, kk:kk + T] * w[kk]
    return jax.nn.silu(y + b), up[:, T:]


def ssd_chunked(xs, dt, A, Bm, Cm, h0):
    Bsz, T = xs.shape[:2]
    L = min(SSD_CHUNK, T)
    nc = T // L

    def to_chunks(a):
        return jnp.moveaxis(a.reshape((Bsz, nc, L) + a.shape[2:]), 1, 0)

    causal = jnp.tril(jnp.ones((L, L), dtype=bool))

    def step(h, inp):
        xc, dtc, Bc, Cc = inp
        cum = jnp.cumsum(dtc * A, axis=1)
        seg = cum[:, :, None] - cum[:, None, :]
        decay = jnp.exp(jnp.where(causal[None, :, :, None, None], seg, -jnp.inf))
        cb = jnp.einsum('blgn,bsgn->blsg', Cc, Bc).astype(jnp.float32)
        m = cb[..., None] * decay * dtc[:, None]
        y_diag = jnp.einsum('blsgr,bsgrp->blgrp', m, xc.astype(jnp.float32))
        y_off = jnp.einsum('blgn,bgrpn->blgrp', Cc.astype(jnp.float32), h) * jnp.exp(cum)[..., None]
        w_end = jnp.exp(cum[:, -1:] - cum) * dtc
        h_new = h * jnp.exp(cum[:, -1])[..., None, None] + jnp.einsum(
            'blgn,blgr,blgrp->bgrpn', Bc.astype(jnp.float32), w_end, xc.astype(jnp.float32))
        return h_new, y_diag + y_off

    h_last, ys = lax.scan(step, h0, (to_chunks(xs), to_chunks(dt), to_chunks(Bm), to_chunks(Cm)))
    y = jnp.moveaxis(ys, 0, 1).reshape(xs.shape)
    return y, h_last


def mamba2_mixer(x, conv_prev, h_prev, w_in, conv_w, conv_b, dt_bias, A_log, d_skip, norm_g, w_out):
    Bsz, T, _ = x.shape
    G, R, P, N = SSM_GROUPS, SSM_HEADS_PER_GROUP, SSM_HEAD_DIM, D_STATE
    proj = x @ w_in
    z = proj[..., :D_INNER]
    xbc = proj[..., D_INNER:D_INNER + CONV_DIM]
    dt_raw = proj[..., D_INNER + CONV_DIM:]
    xbc, conv_new = causal_dwconv(xbc, conv_prev, conv_w, conv_b)
    xs = xbc[..., :D_INNER].reshape(Bsz, T, G, R, P)
    Bm = xbc[..., D_INNER:D_INNER + G * N].reshape(Bsz, T, G, N)
    Cm = xbc[..., D_INNER + G * N:].reshape(Bsz, T, G, N)
    dt = jax.nn.softplus(dt_raw.astype(jnp.float32) + dt_bias.astype(jnp.float32)).reshape(Bsz, T, G, R)
    A = -jnp.exp(A_log.astype(jnp.float32)).reshape(G, R)
    h0 = h_prev.astype(jnp.float32).reshape(Bsz, G, R, P, N)
    y, h_last = ssd_chunked(xs, dt, A, Bm, Cm, h0)
    y = y + d_skip.astype(jnp.float32).reshape(G, R)[:, :, None] * xs.astype(jnp.float32)
    y = y.reshape(Bsz, T, D_INNER) * jax.nn.silu(z.astype(jnp.float32))
    yg = y.reshape(Bsz, T, G, D_INNER // G)
    yg = yg * lax.rsqrt(jnp.mean(yg * yg, axis=-1, keepdims=True) + RMS_EPS)
    y = yg.reshape(Bsz, T, D_INNER) * norm_g.astype(jnp.float32)
    out = y.astype(x.dtype) @ w_out
    return out.astype(x.dtype), conv_new, h_last.reshape(Bsz, SSM_HEADS, P, N)


def stick_breaking_attention(q, k, v, q_offset):
    Bsz, Tq, H, Dh = q.shape
    Tk = k.shape[1]
    qb = min(SB_BLOCK, Tq)
    nb = Tq // qb
    kb = SB_BLOCK
    nk_total = -(-Tk // kb)
    pad = nk_total * kb - Tk
    if pad:
        k = jnp.pad(k, ((0, 0), (0, pad), (0, 0), (0, 0)))
        v = jnp.pad(v, ((0, 0), (0, pad), (0, 0), (0, 0)))
    scale = Dh ** -0.5
    idx = jnp.arange(kb)
    upper_in = (idx[:, None] > idx[None, :]).astype(jnp.float32)
    outs = []
    for i in range(nb):
        s0 = q_offset + i * qb
        nk = min(nk_total, -(-(s0 + qb) // kb))
        kk = k[:, :nk * kb].reshape(Bsz, nk, kb, H, Dh)
        vv = v[:, :nk * kb].reshape(Bsz, nk, kb, H, Dh)
        qblk = q[:, i * qb:(i + 1) * qb]
        z = jnp.einsum('bqhd,bnjhd->bhqnj', qblk, kk).astype(jnp.float32) * scale
        q_pos = s0 + jnp.arange(qb)
        key_pos = jnp.arange(nk * kb).reshape(nk, kb)
        mask = key_pos[None, :, :] < q_pos[:, None, None]
        lk = jnp.where(mask, jax.nn.log_sigmoid(-z), 0.0)
        intra = jnp.einsum('bhqnj,js->bhqns', lk, upper_in)
        bidx = jnp.arange(nk)
        upper_blk = (bidx[:, None] > bidx[None, :]).astype(jnp.float32)
        suffix = jnp.einsum('bhqn,nm->bhqm', lk.sum(-1), upper_blk)
        w = jnp.where(mask, jnp.exp(jax.nn.log_sigmoid(z) + intra + suffix[..., None]), 0.0)
        outs.append(jnp.einsum('bhqnj,bnjhd->bqhd', w, vv.astype(jnp.float32)).astype(q.dtype))
    return jnp.concatenate(outs, axis=1) if nb > 1 else outs[0]


def run_trunk(x, p, conv_prev, ssm_prev, k_past, v_past, q_offset, prm):
    Bsz, T, _ = x.shape
    conv_states = []
    ssm_states = []
    k_all = None
    v_all = None
    k_new = None
    v_new = None
    for i in range(DEPTH):
        if i < N_A:
            mix, cs, hs = mamba2_mixer(x, conv_prev[i], ssm_prev[i], prm['a_w_in'][i], prm['a_conv_w'][i],
                                       prm['a_conv_b'][i], prm['a_dt_bias'][i], prm['a_A_log'][i],
                                       prm['a_d_skip'][i], prm['a_norm_g'][i], prm['a_w_out'][i])
            conv_states.append(cs)
            ssm_states.append(hs)
        else:
            j = i - N_A
            q = (x @ prm['b_w_q'][j]).reshape(Bsz, T, SB_HEADS, SB_HEAD_DIM)
            o = stick_breaking_attention(q, k_all, v_all, q_offset)
            mix = (o.reshape(Bsz, T, SB_WIDTH) @ prm['b_w_out'][j]).astype(x.dtype)
        x = layer_norm(ALPHA * x + mix, prm['ln1_g'][i], prm['ln1_b'][i])
        h = jnp.square(jax.nn.relu(x @ prm['mlp_w1'][i]))
        x = layer_norm(ALPHA * x + (h @ prm['mlp_w2'][i]).astype(x.dtype), prm['ln2_g'][i], prm['ln2_b'][i])
        x = x + (jax.nn.sigmoid(x @ prm['ple_gate_w'][i]) * (p[i] @ prm['ple_w'][i])).astype(x.dtype)
        if i == N_A - 1:
            kv = layer_norm(x, prm['kv_norm_g'], prm['kv_norm_b']) @ prm['w_kv']
            k_new = kv[..., :SB_WIDTH].reshape(Bsz, T, SB_HEADS, SB_HEAD_DIM)
            v_new = kv[..., SB_WIDTH:].reshape(Bsz, T, SB_HEADS, SB_HEAD_DIM)
            if k_past is None:
                k_all, v_all = k_new, v_new
            else:
                k_all = jnp.concatenate([k_past.astype(k_new.dtype), k_new], axis=1)
                v_all = jnp.concatenate([v_past.astype(v_new.dtype), v_new], axis=1)
    return x, jnp.stack(ssm_states), jnp.stack(conv_states), k_new, v_new


def setup_inputs(seed: int = 0) -> dict:
    key = jax.random.key(seed)
    ks = jax.random.split(key, 32)
    f32 = jnp.float32

    def nrm(k, shape, fan_in, s=1.0):
        return jax.random.normal(k, shape, f32) * (s * fan_in ** -0.5)

    u = jax.random.uniform(ks[10], (N_A, SSM_HEADS), f32)
    dt0 = jnp.exp(u * (jnp.log(0.1) - jnp.log(0.001)) + jnp.log(0.001))
    return {
        'x_prompt': jax.random.normal(ks[0], (BATCH, SEQ, D_MODEL), f32),
        'x_sample': jax.random.normal(ks[1], (DEC_BATCH, DEC_SEQ, D_MODEL), f32),
        'p_prompt': jax.random.normal(ks[2], (DEPTH, BATCH, SEQ, PLE_DIM), f32),
        'p_sample': jax.random.normal(ks[3], (DEPTH, DEC_BATCH, DEC_SEQ, PLE_DIM), f32),
        'state_ssm': 0.5 * jax.random.normal(ks[4], (N_A, DEC_BATCH, SSM_HEADS, SSM_HEAD_DIM, D_STATE), f32),
        'state_conv': jax.random.normal(ks[5], (N_A, DEC_BATCH, CONV_W - 1, CONV_DIM), f32),
        'cache_k': jax.random.normal(ks[6], (DEC_BATCH, PAST_LEN, SB_HEADS, SB_HEAD_DIM), f32),
        'cache_v': jax.random.normal(ks[7], (DEC_BATCH, PAST_LEN, SB_HEADS, SB_HEAD_DIM), f32),
        'a_w_in': nrm(ks[8], (N_A, D_MODEL, D_IN_PROJ), D_MODEL),
        'a_conv_w': nrm(ks[9], (N_A, CONV_W, CONV_DIM), CONV_W),
        'a_conv_b': 0.02 * jax.random.normal(ks[11], (N_A, CONV_DIM), f32),
        'a_dt_bias': dt0 + jnp.log(-jnp.expm1(-dt0)),
        'a_A_log': jnp.log(jax.random.uniform(ks[12], (N_A, SSM_HEADS), f32, 1.0, 16.0)),
        'a_d_skip': 1.0 + 0.1 * jax.random.normal(ks[13], (N_A, SSM_HEADS), f32),
        'a_norm_g': 1.0 + 0.02 * jax.random.normal(ks[14], (N_A, D_INNER), f32),
        'a_w_out': nrm(ks[15], (N_A, D_INNER, D_MODEL), D_INNER, BETA),
        'kv_norm_g': 1.0 + 0.02 * jax.random.normal(ks[16], (D_MODEL,), f32),
        'kv_norm_b': 0.02 * jax.random.normal(ks[17], (D_MODEL,), f32),
        'w_kv': nrm(ks[18], (D_MODEL, 2 * SB_WIDTH), D_MODEL),
        'b_w_q': nrm(ks[19], (N_B, D_MODEL, SB_WIDTH), D_MODEL),
        'b_w_out': nrm(ks[20], (N_B, SB_WIDTH, D_MODEL), SB_WIDTH, BETA),
        'ln1_g': 1.0 + 0.02 * jax.random.normal(ks[21], (DEPTH, D_MODEL), f32),
        'ln1_b': 0.02 * jax.random.normal(ks[22], (DEPTH, D_MODEL), f32),
        'ln2_g': 1.0 + 0.02 * jax.random.normal(ks[23], (DEPTH, D_MODEL), f32),
        'ln2_b': 0.02 * jax.random.normal(ks[24], (DEPTH, D_MODEL), f32),
        'mlp_w1': nrm(ks[25], (DEPTH, D_MODEL, D_FF), D_MODEL),
        'mlp_w2': nrm(ks[26], (DEPTH, D_FF, D_MODEL), D_FF, BETA),
        'ple_w': nrm(ks[27], (DEPTH, PLE_DIM, D_MODEL), PLE_DIM),
        'ple_gate_w': nrm(ks[28], (DEPTH, D_MODEL, D_MODEL), D_MODEL),
    }


def reference(x_prompt, x_sample, p_prompt, p_sample, state_ssm, state_conv, cache_k, cache_v,
              a_w_in, a_conv_w, a_conv_b, a_dt_bias, a_A_log, a_d_skip, a_norm_g, a_w_out,
              kv_norm_g, kv_norm_b, w_kv, b_w_q, b_w_out, ln1_g, ln1_b, ln2_g, ln2_b,
              mlp_w1, mlp_w2, ple_w, ple_gate_w):
    prm = {
        'a_w_in': a_w_in, 'a_conv_w': a_conv_w, 'a_conv_b': a_conv_b, 'a_dt_bias': a_dt_bias,
        'a_A_log': a_A_log, 'a_d_skip': a_d_skip, 'a_norm_g': a_norm_g, 'a_w_out': a_w_out,
        'kv_norm_g': kv_norm_g, 'kv_norm_b': kv_norm_b, 'w_kv': w_kv,
        'b_w_q': b_w_q, 'b_w_out': b_w_out,
        'ln1_g': ln1_g, 'ln1_b': ln1_b, 'ln2_g': ln2_g, 'ln2_b': ln2_b,
        'mlp_w1': mlp_w1, 'mlp_w2': mlp_w2, 'ple_w': ple_w, 'ple_gate_w': ple_gate_w,
    }
    Bp = x_prompt.shape[0]
    conv0 = jnp.zeros((N_A, Bp, CONV_W - 1, CONV_DIM), x_prompt.dtype)
    ssm0 = jnp.zeros((N_A, Bp, SSM_HEADS, SSM_HEAD_DIM, D_STATE), jnp.float32)
    y_prompt, ssm_p, conv_p, k_p, v_p = run_trunk(x_prompt, p_prompt, conv0, ssm0, None, None, 0, prm)
    y_sample, ssm_s, conv_s, k_s, v_s = run_trunk(x_sample, p_sample, state_conv, state_ssm,
                                                  cache_k, cache_v, PAST_LEN, prm)
    return (y_prompt, y_sample, ssm_p, conv_p, k_p, v_p, ssm_s, conv_s, k_s, v_s)
```

```python
from contextlib import ExitStack
import os
import numpy as np
import ml_dtypes
import concourse.bass as bass
import concourse.mybir as mybir
from concourse.bass_utils import run_bass_kernel_spmd

F32 = mybir.dt.float32
BF16 = mybir.dt.bfloat16
AF = mybir.ActivationFunctionType
ALU = mybir.AluOpType

D = 1024
DI = 2048
NH = 32
HP = 64
NG = 8
NS = 128
CONVD = 4096
DINP = 6176
DFF = 4096
PLE = 256
SBW = 512
PAST = 1024
DEPTH = 4
ALPHA = (2.0 * DEPTH) ** 0.25
LN_EPS = 1e-5
RMS_EPS = 1e-5
SEM_ROT = 30000
PIECE = 4096
NSLOT = 4


class Op:
    __slots__ = ("eng", "fn", "dma", "deps", "signal", "sig", "idx", "dkey", "persist")

    def __init__(self, eng, fn, dma, dkey, persist):
        self.eng = eng; self.fn = fn; self.dma = dma; self.deps = []; self.signal = False
        self.sig = None; self.dkey = dkey; self.persist = persist


class Sched:
    def __init__(self, nc):
        self.nc = nc
        self.ops = []
        self.last_w = {}
        self.readers = {}
        self.last_eng = {}
        self.dma_since_bar = []
        self.pending = {}
        self.bar_deps = []

    def op(self, eng, fn, reads=(), writes=(), dma=False, dkey=None, persist=False):
        o = Op(eng, fn, dma, dkey, persist)
        o.idx = len(self.ops)
        deps = {}
        for k in reads:
            w = self.last_w.get(k)
            if w is not None:
                deps[w.idx] = (w, True)
            if k.startswith("ps"):
                for r in self.readers.get(k, ()):
                    if r.eng != eng and r.idx not in deps:
                        deps[r.idx] = (r, True)
        for k in writes:
            w = self.last_w.get(k)
            if w is not None and w.idx not in deps:
                deps[w.idx] = (w, False)
            for r in self.readers.get(k, ()):
                if r.idx not in deps:
                    deps[r.idx] = (r, False)
        for (d, raw) in deps.values():
            if d.eng == eng and not dma and not d.dma:
                if not raw or eng == "pe":
                    continue
            o.deps.append(d)
            d.signal = True
        if not dma and eng in self.pending:
            for d in self.pending.pop(eng):
                if d.dma or d.eng != eng:
                    o.deps.append(d)
        for k in writes:
            self.last_w[k] = o
            self.readers[k] = []
        for k in reads:
            if k not in writes:
                self.readers.setdefault(k, []).append(o)
        self.ops.append(o)
        if dma:
            o.signal = True
            if not persist:
                self.dma_since_bar.append(o)
                for d in self.bar_deps:
                    if d.dma or d.eng != eng:
                        o.deps.append(d)
        else:
            self.last_eng[eng] = o
        return o

    def pe(self, fn, reads=(), writes=()): return self.op("pe", fn, reads, writes)
    def act(self, fn, reads=(), writes=()): return self.op("act", fn, reads, writes)
    def dve(self, fn, reads=(), writes=()): return self.op("dve", fn, reads, writes)
    def pool(self, fn, reads=(), writes=()): return self.op("pool", fn, reads, writes)

    def dma(self, out, in_, reads, writes, dkey, q="sp", persist=False, **kw):
        return self.op(q, lambda e: e.dma_start(out=out, in_=in_, **kw), reads, writes, dma=True,
                       dkey=dkey + ("_sw" if q == "pool" else ""), persist=persist)

    def barrier(self):
        deps = list(self.last_eng.values()) + list(self.dma_since_bar)
        self.dma_since_bar = []
        for d in deps:
            d.signal = True
        self.pending = {eng: deps for eng in ("pe", "act", "dve", "pool")}
        self.bar_deps = deps

    def finish(self):
        nc = self.nc
        ops = self.ops
        sem_names = {}
        eng_cnt = {}
        dk_cnt = {}
        per = SEM_ROT // 16
        for o in ops:
            if not o.signal:
                continue
            if o.dma:
                c = dk_cnt.get(o.dkey, 0) + 1
                dk_cnt[o.dkey] = c
                rot = (c - 1) // per
                o.sig = (f"d_{o.dkey}_{rot}", 16 * (c - rot * per))
            else:
                c = eng_cnt.get(o.eng, 0) + 1
                eng_cnt[o.eng] = c
                rot = (c - 1) // SEM_ROT
                o.sig = (f"e_{o.eng}_{rot}", c - rot * SEM_ROT)
            sem_names[o.sig[0]] = None
        last_dma_sig = {}
        for o in ops:
            if o.dma:
                last_dma_sig[o.sig[0]] = max(last_dma_sig.get(o.sig[0], 0), o.sig[1])
        by_eng = {}
        for o in ops:
            by_eng.setdefault(o.eng, []).append(o)
        self.n_ops = len(ops)
        with ExitStack() as es:
            sems = {n: es.enter_context(nc.semaphore(n)) for n in sem_names}
            block = es.enter_context(nc.Block())

            def emit(engname):
                def body(e):
                    waited = {}
                    for o in by_eng.get(engname, ()):
                        need = {}
                        for d in o.deps:
                            sn, sv = d.sig
                            if sv > need.get(sn, 0):
                                need[sn] = sv
                        for sn, sv in need.items():
                            if waited.get(sn, 0) >= sv:
                                continue
                            e.wait_ge(sems[sn], sv)
                            waited[sn] = sv
                        ins = o.fn(e)
                        if o.signal:
                            ins.then_inc(sems[o.sig[0]], 16 if o.dma else 1)
                    if engname == "sp":
                        for sn, sv in last_dma_sig.items():
                            if waited.get(sn, 0) < sv:
                                e.wait_ge(sems[sn], sv)
                return body

            block.sync(emit("sp"))
            block.scalar(emit("act"))
            block.vector(emit("dve"))
            block.gpsimd(emit("pool"))
            block.tensor(emit("pe"))


DBG = {}


class StopBuild(Exception):
    pass


def build(SEQ, NSEQ_S=2, TS=16, KSTOP=None):
    nc = bass.Bass("TRN2", target_bir_lowering=False)
    stg_ = {"i": 0}

    def stage(name):
        stg_["i"] += 1
        if KSTOP is not None and stg_["i"] >= KSTOP:
            print("STOP at stage", stg_["i"], name)
            raise StopBuild()

    S = Sched(nc)

    def din(name, shape, dt=F32):
        return nc.dram_tensor(name, list(shape), dt, kind="ExternalInput").ap()

    def dout(name, shape, dt=F32):
        return nc.dram_tensor(name, list(shape), dt, kind="ExternalOutput").ap()

    x_p = din("x_p", [SEQ, D]); x_s = din("x_s", [NSEQ_S, TS, D])
    p_p = din("p_p", [DEPTH, SEQ, PLE]); p_s = din("p_s", [DEPTH, NSEQ_S, TS, PLE])
    st_ssm = din("st_ssm", [2, NSEQ_S, NH * HP, NS]); st_conv = din("st_conv", [2, NSEQ_S, 3, CONVD])
    c_k = din("c_k", [NSEQ_S, PAST, SBW]); c_v = din("c_v", [NSEQ_S, PAST, SBW])
    a_w_in = din("a_w_in", [2, D, DINP]); a_conv_w = din("a_conv_w", [2, 4, CONVD]); a_conv_b = din("a_conv_b", [2, CONVD])
    a_dt_bias = din("a_dt_bias", [2, NH]); a_A_log = din("a_A_log", [2, NH]); a_d_skip = din("a_d_skip", [2, NH])
    a_norm_g = din("a_norm_g", [2, DI]); a_w_out = din("a_w_out", [2, DI, D])
    kv_g = din("kv_norm_g", [D]); kv_b = din("kv_norm_b", [D]); w_kv = din("w_kv", [D, 2 * SBW])
    b_w_q = din("b_w_q", [2, D, SBW]); b_w_out = din("b_w_out", [2, SBW, D])
    ln1_g = din("ln1_g", [DEPTH, D]); ln1_b = din("ln1_b", [DEPTH, D]); ln2_g = din("ln2_g", [DEPTH, D]); ln2_b = din("ln2_b", [DEPTH, D])
    mlp_w1 = din("mlp_w1", [DEPTH, D, DFF]); mlp_w2 = din("mlp_w2", [DEPTH, DFF, D])
    ple_w = din("ple_w", [DEPTH, PLE, D]); gate_w = din("ple_gate_w", [DEPTH, D, D])
    NCB = 512 + 2048
    cst_bf_d = din("cst_bf", [128, NCB], BF16); cst_f_d = din("cst_f", [128, 384])

    y_p = dout("y_p", [SEQ, D]); y_s = dout("y_s", [NSEQ_S, TS, D])
    ssm_p = dout("ssm_p", [2, NH * HP, NS]); conv_p = dout("conv_p", [2, 3, CONVD])
    k_p = dout("k_p", [SEQ, SBW]); v_p = dout("v_p", [SEQ, SBW])
    ssm_s = dout("ssm_s", [2, NSEQ_S, NH * HP, NS]); conv_s = dout("conv_s", [2, NSEQ_S, 3, CONVD])
    k_s = dout("k_s", [NSEQ_S, TS, SBW]); v_s = dout("v_s", [NSEQ_S, TS, SBW])

    pieces = {}
    plist = []

    def add_piece(name, src, a, b):
        pieces[name] = (len(plist), a, b)
        plist.append((name, src, a, b))

    def colblock(W, n0, nw):
        return W[:, n0:n0 + nw].rearrange("(k p) n -> p k n", p=128)

    def rowblock(W, k0, kk):
        return W[k0 * 128:(k0 + kk) * 128, :].rearrange("(k p) n -> p k n", p=128)

    for l in range(DEPTH):
        if l < 2:
            for i in range(8):
                add_piece(f"xbc{l}_{i}", colblock(a_w_in[l], DI + i * 512, 512), 8, 512)
            for i in range(4):
                add_piece(f"z{l}_{i}", colblock(a_w_in[l], i * 512, 512), 8, 512)
            add_piece(f"dt{l}", colblock(a_w_in[l], DI + CONVD, NH), 8, NH)
            for i in range(4):
                add_piece(f"wo{l}_{i}", rowblock(a_w_out[l], i * 4, 4), 4, D)
        else:
            add_piece(f"q{l}", colblock(b_w_q[l - 2], 0, SBW), 8, SBW)
            add_piece(f"bo{l}", rowblock(b_w_out[l - 2], 0, 4), 4, D)
        for i in range(8):
            add_piece(f"w1{l}_{i}", colblock(mlp_w1[l], i * 512, 512), 8, 512)
        for i in range(8):
            add_piece(f"w2{l}_{i}", rowblock(mlp_w2[l], i * 4, 4), 4, D)
        for i in range(2):
            add_piece(f"g{l}_{i}", colblock(gate_w[l], i * 512, 512), 8, 512)
        add_piece(f"pw{l}", rowblock(ple_w[l], 0, 2), 2, D)
        if l == 1:
            for i in range(2):
                add_piece(f"kv_{i}", colblock(w_kv, i * 512, 512), 8, 512)
    NP = len(plist)
    wbf = nc.dram_tensor("wbf", [NP, 128, PIECE], BF16, kind="Internal").ap()

    def tile_order():
        o = []
        for l in range(DEPTH):
            if l < 2:
                o += [f"xbc{l}_{i}" for i in range(8)] + [f"z{l}_{i}" for i in range(4)] + [f"dt{l}"]
                o += [f"wo{l}_{i}" for i in range(4)]
            else:
                o += [f"q{l}", f"bo{l}"]
            o += [f"w1{l}_{i}" for i in range(8)] + [f"w2{l}_{i}" for i in range(8)]
            o += [f"g{l}_0", f"g{l}_1", f"pw{l}"]
            if l == 1:
                o += ["kv_0", "kv_1"]
        return o

    tiles = [("p", t * 512, 128, 4) for t in range(SEQ // 512)] + [("s", 0, TS, NSEQ_S)]
    worder = tile_order() * len(tiles)

    KLEN_S = PAST + 128
    kT_p = nc.dram_tensor("kT_p", [128, 4, SEQ], BF16, kind="Internal").ap()
    vv_p = nc.dram_tensor("vv_p", [SEQ, SBW], BF16, kind="Internal").ap()
    kT_s = nc.dram_tensor("kT_s", [NSEQ_S, 128, 4, KLEN_S], BF16, kind="Internal").ap()
    vv_s = nc.dram_tensor("vv_s", [NSEQ_S, KLEN_S, SBW], BF16, kind="Internal").ap()

    with ExitStack() as es:
        ASZ = 207 * 1024
        arena_t = es.enter_context(nc.sbuf_tensor("arena", [128, ASZ], mybir.dt.uint8))
        top = [0]

        def alloc(dt, *shape, p=128):
            n = int(np.prod(shape))
            sz = 4 if dt == F32 else 2
            off = top[0]
            top[0] += (n * sz + 63) // 64 * 64
            assert top[0] <= ASZ, (top[0], ASZ)
            v = arena_t[:, off:off + n * sz].bitcast(dt)
            if len(shape) == 2:
                v = v.rearrange("p (a b) -> p a b", a=shape[0])
            elif len(shape) == 3:
                v = v.rearrange("p (a b c) -> p a b c", a=shape[0], b=shape[1])
            return v

        PS = [es.enter_context(nc.psum_tensor(f"ps{i}", [128, 512], F32))[:, :] for i in range(8)]
        PK = [f"ps{i}" for i in range(8)]

        cbf = alloc(BF16, NCB); cf = alloc(F32, 384)
        ident_bf = cbf[:, 0:128]; GT_bf = cbf[:, 128:256]; LE_bf = cbf[:, 256:384]
        ones_bf = cbf[:, 384:512]
        M4 = cbf[:, 512:512 + 2048].rearrange("p (a b) -> p a b", a=4)
        ident_f = cf[:, 0:128]; LE_f = cf[:, 128:256]; ones_f = cf[:, 256:384]
        cone = alloc(F32, 1); ceps = alloc(F32, 1); crms = alloc(F32, 1)
        wslots = [alloc(BF16, PIECE) for _ in range(NSLOT)]
        x_tok = alloc(F32, 4, D)
        xT = alloc(BF16, 8, 512)
        lng = alloc(F32, D); lnb = alloc(F32, D)
        hT = [alloc(F32, DI) for _ in range(2)]
        hT_bf = alloc(BF16, DI)
        cst = [alloc(F32, 32, 3) for _ in range(2)]
        cw = alloc(F32, 32, 4); cb = alloc(F32, 32)
        dtb = alloc(F32, NH); Aneg = alloc(F32, NH); dsk = alloc(F32, NH)
        mv = alloc(F32, 2); stt = alloc(F32, 2, 6); rstd = alloc(F32, 1); ss = alloc(F32, 1)
        PERSIST_TOP = top[0]

        S.dma(cbf, cst_bf_d, [], ["cbf"], "cbf")
        S.dma(cf, cst_f_d, [], ["cf"], "cf")
        S.pool(lambda e: e.memset(cone, 1.0), [], ["cone"])
        S.pool(lambda e: e.memset(ceps, LN_EPS), [], ["ceps"])
        S.pool(lambda e: e.memset(crms, RMS_EPS), [], ["crms"])
        CON = ["cbf", "cf", "cone", "ceps", "crms"]

        for (name, src, a, b) in plist:
            pid = pieces[name][0]
            dst = wbf[pid][:, 0:a * b].rearrange("p (k n) -> p k n", k=a)
            S.dma(dst, src, [], [f"wbf{pid}"], "wconv", q="pool", persist=True)

        wstate = {"next_load": 0, "next_use": 0}

        def w_issue(upto):
            while wstate["next_load"] < min(upto, len(worder)):
                j = wstate["next_load"]
                name = worder[j]
                pid, a, b = pieces[name]
                sl = j % NSLOT
                rd = [f"wbf{p_}" for p_ in range(NP)] if j == 0 else []
                S.dma(wslots[sl][:, 0:a * b], wbf[pid][:, 0:a * b], rd, [f"wslot{sl}"], f"wslot{sl}", persist=True)
                wstate["next_load"] += 1

        def w_next(name, back=0):
            j = wstate["next_use"]
            assert worder[j] == name, (worder[j], name)
            w_issue(j - back + NSLOT)
            wstate["next_use"] += 1
            pid, a, b = pieces[name]
            sl = j % NSLOT
            return wslots[sl][:, 0:a * b].rearrange("p (k n) -> p k n", k=a), f"wslot{sl}"

        rot = {}

        def psr(lst):
            k = tuple(lst)
            rot[k] = rot.get(k, 0) + 1
            return lst[rot[k] % len(lst)]

        def transpose_to_xT(ct, nch, src_tok, src_key, dstT, dst_key, nkc, tmp_bf, tmp_key):
            for c in range(nch):
                S.pool(lambda e, c=c: e.tensor_copy(out=tmp_bf[:ct, 0:nkc * 128], in_=src_tok[:ct, c, 0:nkc * 128]), [src_key], [tmp_key])
                pi = psr([0, 7])
                pst = PS[pi].bitcast(BF16)
                for k in range(nkc):
                    S.pe(lambda e, k=k, pst=pst: e.transpose(out=pst[:, k * ct:(k + 1) * ct], in_=tmp_bf[:ct, k * 128:(k + 1) * 128], identity=ident_bf[:ct, :ct]),
                         [tmp_key, "cbf"], [PK[pi]])
                S.act(lambda e, c=c, pst=pst: e.activation(out=dstT[:, 0:nkc, c * ct:(c + 1) * ct], in_=pst[:, 0:nkc * ct].rearrange("p (k t) -> p k t", k=nkc), func=AF.Copy),
                      [PK[pi]], [dst_key])

        def layer_norm(ct, c, mix_banks, g_ap, b_ap, gkey, dst, dst_key, vbuf, vkey, src=None, src_key=None, alpha=None):
            if mix_banks is not None:
                for hb in range(2):
                    pi = mix_banks[hb]
                    S.dve(lambda e, hb=hb, pi=pi: e.scalar_tensor_tensor(out=vbuf[:ct, hb * 512:(hb + 1) * 512], in0=x_tok[:ct, c, hb * 512:(hb + 1) * 512], scalar=float(alpha),
                                                                          in1=PS[pi][:ct, :], op0=ALU.mult, op1=ALU.add), ["x_tok", PK[pi]], [vkey])
                v = vbuf[:ct, :]; vk = vkey
            else:
                v = src; vk = src_key
            for hb in range(2):
                S.dve(lambda e, hb=hb: e.bn_stats(out=stt[:ct, hb, :], in_=v[:, hb * 512:(hb + 1) * 512]), [vk], ["stt"])
            S.dve(lambda e: e.bn_aggr(out=mv[:ct, :], in_=stt[:ct, :, :]), ["stt"], ["mv"])
            S.act(lambda e: e.activation(out=rstd[:ct, :], in_=mv[:ct, 1:2], func=AF.Ln, bias=ceps[:ct, :], scale=1.0), ["mv", "ceps"], ["rstd"])
            S.act(lambda e: e.activation(out=rstd[:ct, :], in_=rstd[:ct, :], func=AF.Exp, scale=-0.5), ["rstd"], ["rstd"])
            S.dve(lambda e: e.tensor_scalar(out=vbuf[:ct, :], in0=v, scalar1=mv[:ct, 0:1], scalar2=rstd[:ct, :], op0=ALU.subtract, op1=ALU.mult),
                  [vk, "mv", "rstd"], [vkey])
            S.pool(lambda e: e.tensor_tensor(out=vbuf[:ct, :], in0=vbuf[:ct, :], in1=g_ap[:ct, :], op=ALU.mult), [vkey, gkey], [vkey])
            S.pool(lambda e: e.tensor_tensor(out=dst, in0=vbuf[:ct, :], in1=b_ap[:ct, :], op=ALU.add), [vkey, gkey], [dst_key])

        def load_ln(g_d, b_d):
            S.dma(lng, g_d.partition_broadcast(128), [], ["lnp"], "lnp")
            S.dma(lnb, b_d.partition_broadcast(128), [], ["lnp"], "lnp")

        top[0] = PERSIST_TOP
        ktok = alloc(BF16, SBW); kst = alloc(BF16, 4, 128)
        zpad = alloc(BF16, SBW)
        vtok_ = alloc(BF16, SBW)
        S.pool(lambda e: e.memset(zpad, 0.0), [], ["zpad"])
        for s in range(NSEQ_S):
            S.dma(kT_s[s][:, :, PAST:PAST + 128], zpad.rearrange("p (h t) -> p h t", h=4), ["zpad"], [f"kT_s{s}"], f"zk{s}", q="pool")
            S.dma(vv_s[s][PAST:PAST + 128, :], zpad, ["zpad"], [f"vv_s{s}"], f"zv{s}", q="pool")
            for kb in range(PAST // 128):
                S.dma(vtok_, c_v[s][kb * 128:(kb + 1) * 128, :], [], ["vtok_"], "vtok_", q="pool")
                S.dma(vv_s[s][kb * 128:(kb + 1) * 128, :], vtok_, ["vtok_"], [f"vv_s{s}"], "vtok_o", q="pool")
                S.dma(ktok, c_k[s][kb * 128:(kb + 1) * 128, :], [], ["ktok"], "ktok", q="pool")
                pst = PS[0].bitcast(BF16)
                for h in range(4):
                    S.pe(lambda e, h=h, pst=pst: e.transpose(out=pst[:, h * 128:(h + 1) * 128], in_=ktok[:, h * 128:(h + 1) * 128], identity=ident_bf), ["ktok", "cbf"], [PK[0]])
                S.act(lambda e, pst=pst: e.activation(out=kst, in_=pst[:, 0:512].rearrange("p (h t) -> p h t", h=4), func=AF.Copy), [PK[0]], ["kst"])
                S.dma(kT_s[s][:, :, kb * 128:(kb + 1) * 128], kst, ["kst"], [f"kT_s{s}"], "kst", q="pool")
        S.barrier()

        def mamba_mixer(l, kind, t0, ct, nch, tix):
            TT = ct * nch
            nsq, L = (1, TT) if kind == "p" else (nch, ct)
            first = (kind == "p" and t0 == 0)
            last = (kind == "p" and t0 + TT == SEQ)
            top[0] = PERSIST_TOP
            convin = [alloc(F32, nsq, 3 + L) for _ in range(2)]
            acc = [alloc(F32, nsq, L) for _ in range(2)]
            xcT = alloc(BF16, 16, 512)
            BT = alloc(BF16, 8, 512); CT = alloc(BF16, 8, 512)
            xtc = alloc(BF16, 4, DI); Btok = alloc(BF16, 4, NG * NS)
            sz = alloc(BF16, 4, DI)
            normg = alloc(F32, 16)
            dtt = alloc(F32, 4, NH); dtA = alloc(F32, 4, NH); dthi = alloc(BF16, 4, NH); dtlo = alloc(BF16, 4, NH); tmp32 = alloc(F32, NH)
            cum = alloc(F32, NH); ecum = alloc(F32, NH); etot = alloc(F32, NH); wend = alloc(F32, NH)
            xw = alloc(BF16, DI)
            Rhi = [alloc(BF16, 4, 128) for _ in range(2)]; Rlo = [alloc(BF16, 4, 128) for _ in range(2)]
            E = alloc(F32, 4, 128); cbm = alloc(F32, 128); mT = alloc(BF16, 4, 128)
            yo = alloc(F32, 256); y1 = alloc(F32, 256); yn = alloc(BF16, 256); junk = alloc(F32, 256)
            stg = alloc(F32, 16, 128)
            cstl = alloc(F32, nsq, 32, 3)
            yT = xcT

            for k_ in range(4):
                S.dma(cw[:, :, k_], a_conv_w[l, k_].rearrange("(c p) -> p c", p=128), [], ["cw"], "cw", allow_slow_non_contiguous=True)
            S.dma(cb, a_conv_b[l].rearrange("(c p) -> p c", p=128), [], ["cb"], "cb", allow_slow_non_contiguous=True)
            S.dma(dtb, a_dt_bias[l].partition_broadcast(128), [], ["dtb"], "dtb")
            S.dma(Aneg, a_A_log[l].partition_broadcast(128), [], ["Aneg"], "Aneg")
            S.dma(dsk, a_d_skip[l].partition_broadcast(128), [], ["dsk"], "dsk")
            S.dma(normg, a_norm_g[l].rearrange("(c p) -> p c", p=128), [], ["normg"], "normg", allow_slow_non_contiguous=True)
            S.act(lambda e: e.activation(out=Aneg, in_=Aneg, func=AF.Exp), ["Aneg"], ["Aneg"])
            S.dve(lambda e: e.tensor_scalar(out=Aneg, in0=Aneg, scalar1=-1.0, scalar2=None, op0=ALU.mult), ["Aneg"], ["Aneg"])
            load_ln(ln1_g[l], ln1_b[l])
            if kind == "s":
                for s_ in range(nsq):
                    for k_ in range(3):
                        for q4 in range(2):
                            S.dma(cstl[:, s_, q4 * 16:(q4 + 1) * 16, k_], st_conv[l, s_, k_][q4 * 2048:(q4 + 1) * 2048].rearrange("(c p) -> p c", p=128),
                                  [], ["cstl"], "cstl", allow_slow_non_contiguous=True)
            elif first:
                S.pool(lambda e: e.memset(cst[l], 0.0), [], [f"cst{l}"])

            for i in range(8):
                wv, wk = w_next(f"xbc{l}_{i}")
                for j in range(4):
                    cc = i * 4 + j
                    pi = psr([1, 2])
                    for kc in range(8):
                        S.pe(lambda e, kc=kc, j=j, wv=wv, pi=pi: e.matmul(PS[pi][:, 0:TT], lhsT=wv[:, kc, j * 128:(j + 1) * 128], rhs=xT[:, kc, 0:TT], start=(kc == 0), stop=(kc == 7)),
                             [wk, "xT"], [PK[pi]])
                    b = cc % 2
                    ci = convin[b]; ck = f"convin{b}"; ac = acc[b]; ak = f"acc{b}"
                    S.act(lambda e, ci=ci, pi=pi: e.activation(out=ci[:, :, 3:3 + L], in_=PS[pi][:, 0:TT].rearrange("p (s t) -> p s t", s=nsq), func=AF.Copy), [PK[pi]], [ck])
                    if kind == "p":
                        S.pool(lambda e, ci=ci, cc=cc: e.tensor_copy(out=ci[:, 0, 0:3], in_=cst[l][:, cc, :]), [f"cst{l}"], [ck])
                    else:
                        S.pool(lambda e, ci=ci, cc=cc: e.tensor_copy(out=ci[:, :, 0:3], in_=cstl[:, :, cc, :]), ["cstl"], [ck])
                    S.dve(lambda e, ci=ci, ac=ac, cc=cc: e.tensor_scalar(out=ac, in0=ci[:, :, 0:L], scalar1=cw[:, cc, 0:1], scalar2=None, op0=ALU.mult), [ck, "cw"], [ak])
                    for k in range(1, 4):
                        S.dve(lambda e, ci=ci, ac=ac, cc=cc, k=k: e.scalar_tensor_tensor(out=ac, in0=ci[:, :, k:k + L], scalar=cw[:, cc, k:k + 1], in1=ac, op0=ALU.mult, op1=ALU.add),
                              [ck, "cw", ak], [ak])
                    if cc < 16:
                        dst = xcT[:, cc, 0:TT]; dk = "xcT"
                    elif cc < 24:
                        dst = BT[:, cc - 16, 0:TT]; dk = "BT"
                    else:
                        dst = CT[:, cc - 24, 0:TT]; dk = "CT"
                    S.act(lambda e, ac=ac, dst=dst, cc=cc: e.activation(out=dst.rearrange("p (s t) -> p s t", s=nsq), in_=ac, func=AF.Silu, bias=cb[:, cc:cc + 1], scale=1.0), [ak, "cb"], [dk])
                    if kind == "p":
                        S.pool(lambda e, ci=ci, cc=cc: e.tensor_copy(out=cst[l][:, cc, :], in_=ci[:, 0, L:L + 3]), [ck], [f"cst{l}"])
                    else:
                        S.pool(lambda e, ci=ci, cc=cc: e.tensor_copy(out=cstl[:, :, cc, :], in_=ci[:, :, L:L + 3]), [ck], ["cstl"])
            if kind == "s":
                for s_ in range(nsq):
                    for k_ in range(3):
                        for q4 in range(2):
                            S.dma(conv_s[l, s_, k_][q4 * 2048:(q4 + 1) * 2048].rearrange("(c p) -> p c", p=128), cstl[:, s_, q4 * 16:(q4 + 1) * 16, k_],
                                  ["cstl"], ["conv_out"], "cstl", q="pool", allow_slow_non_contiguous=True)
            elif last:
                for k_ in range(3):
                    for q4 in range(2):
                        S.dma(conv_p[l, k_][q4 * 2048:(q4 + 1) * 2048].rearrange("(c p) -> p c", p=128), cst[l][:, q4 * 16:(q4 + 1) * 16, k_],
                              [f"cst{l}"], ["conv_out"], f"cst{l}", q="pool", allow_slow_non_contiguous=True)

            for c in range(nch):
                for half in range(2):
                    pi = psr([3, 4])
                    pst = PS[pi].bitcast(BF16)
                    for j in range(8):
                        S.pe(lambda e, j=j, pst=pst, c=c, half=half: e.transpose(out=pst[:ct, j * 128:(j + 1) * 128], in_=xcT[:, half * 8 + j, c * ct:(c + 1) * ct], identity=ident_bf),
                             ["xcT", "cbf"], [PK[pi]])
                    S.act(lambda e, pst=pst, c=c, half=half: e.activation(out=xtc[:ct, c, half * 1024:(half + 1) * 1024], in_=pst[:ct, 0:1024], func=AF.Copy), [PK[pi]], ["xtc"])
                pi = psr([3, 4])
                pst = PS[pi].bitcast(BF16)
                for j in range(8):
                    S.pe(lambda e, j=j, pst=pst, c=c: e.transpose(out=pst[:ct, j * 128:(j + 1) * 128], in_=BT[:, j, c * ct:(c + 1) * ct], identity=ident_bf), ["BT", "cbf"], [PK[pi]])
                S.act(lambda e, pst=pst, c=c: e.activation(out=Btok[:ct, c, :], in_=pst[:ct, 0:1024], func=AF.Copy), [PK[pi]], ["Btok"])

            for i in range(4):
                wv, wk = w_next(f"z{l}_{i}")
                for c in range(nch):
                    pi = psr([5, 6])
                    for kc in range(8):
                        S.pe(lambda e, kc=kc, c=c, wv=wv, pi=pi: e.matmul(PS[pi][:ct, :], lhsT=xT[:, kc, c * ct:(c + 1) * ct], rhs=wv[:, kc, :], start=(kc == 0), stop=(kc == 7)),
                             [wk, "xT"], [PK[pi]])
                    S.act(lambda e, c=c, i=i, pi=pi: e.activation(out=sz[:ct, c, i * 512:(i + 1) * 512], in_=PS[pi][:ct, :], func=AF.Silu), [PK[pi]], ["sz"])
            wv, wk = w_next(f"dt{l}")
            for c in range(nch):
                pi = psr([5, 6])
                for kc in range(8):
                    S.pe(lambda e, kc=kc, c=c, wv=wv, pi=pi: e.matmul(PS[pi][:ct, 0:NH], lhsT=xT[:, kc, c * ct:(c + 1) * ct], rhs=wv[:, kc, :], start=(kc == 0), stop=(kc == 7)),
                         [wk, "xT"], [PK[pi]])
                S.dve(lambda e, c=c, pi=pi: e.tensor_tensor(out=dtt[:ct, c, :], in0=PS[pi][:ct, 0:NH], in1=dtb[:ct, :], op=ALU.add), [PK[pi], "dtb"], ["dtt"])
            S.act(lambda e: e.activation(out=dtt[:ct, 0:nch, :], in_=dtt[:ct, 0:nch, :], func=AF.Exp), ["dtt"], ["dtt"])
            S.act(lambda e: e.activation(out=dtt[:ct, 0:nch, :], in_=dtt[:ct, 0:nch, :], func=AF.Ln, bias=cone[:ct, :], scale=1.0), ["dtt", "cone"], ["dtt"])
            S.dve(lambda e: e.tensor_tensor(out=dtA[:ct, 0:nch, :], in0=dtt[:ct, 0:nch, :], in1=Aneg[:ct, :].unsqueeze(1).broadcast_to([ct, nch, NH]), op=ALU.mult), ["dtt", "Aneg"], ["dtA"])
            S.dve(lambda e: e.tensor_copy(out=dthi[:ct, 0:nch, :], in_=dtA[:ct, 0:nch, :]), ["dtA"], ["dthi"])
            S.dve(lambda e: e.tensor_tensor(out=dtlo[:ct, 0:nch, :], in0=dtA[:ct, 0:nch, :], in1=dthi[:ct, 0:nch, :], op=ALU.subtract), ["dtA", "dthi"], ["dtlo"])

            hk = f"hT{l}"
            for c in range(nch):
                if kind == "s":
                    S.dma(stg, st_ssm[l, c].rearrange("(j q) n -> q j n", q=128), [], ["stg"], "stg")
                    for j in range(16):
                        pi = psr([3, 4])
                        S.pe(lambda e, j=j, pi=pi: e.transpose(out=PS[pi][:, 0:128], in_=stg[:, j, :], identity=ident_f), ["stg", "cf"], [PK[pi]])
                        S.act(lambda e, j=j, pi=pi: e.activation(out=hT[l][:, j * 128:(j + 1) * 128], in_=PS[pi][:, 0:128], func=AF.Copy), [PK[pi]], [hk])
                elif first and c == 0:
                    S.pool(lambda e: e.memset(hT[l], 0.0), [], [hk])
                S.pool(lambda e: e.tensor_copy(out=hT_bf, in_=hT[l]), [hk], ["hT_bf"])
                S.pe(lambda e, c=c: e.matmul(PS[7][:ct, 0:NH], lhsT=LE_f[:ct, :ct], rhs=dtA[:ct, c, :], start=True, stop=True), ["cf", "dtA"], [PK[7]])
                S.pe(lambda e, c=c: e.matmul(PS[7][:, NH:2 * NH], lhsT=ones_f[:ct, :], rhs=dtA[:ct, c, :], start=True, stop=True), ["cf", "dtA"], [PK[7]])
                S.act(lambda e: e.activation(out=cum[:ct, :], in_=PS[7][:ct, 0:NH], func=AF.Copy), [PK[7]], ["cum"])
                S.act(lambda e: e.activation(out=ecum[:ct, :], in_=PS[7][:ct, 0:NH], func=AF.Exp), [PK[7]], ["ecum"])
                S.act(lambda e: e.activation(out=etot, in_=PS[7][:, NH:2 * NH], func=AF.Exp), [PK[7]], ["etot"])
                S.dve(lambda e: e.tensor_tensor(out=wend[:ct, :], in0=PS[7][:ct, NH:2 * NH], in1=cum[:ct, :], op=ALU.subtract), [PK[7], "cum"], ["wend"])
                S.act(lambda e: e.activation(out=wend[:ct, :], in_=wend[:ct, :], func=AF.Exp), ["wend"], ["wend"])
                S.dve(lambda e, c=c: e.tensor_tensor(out=wend[:ct, :], in0=wend[:ct, :], in1=dtt[:ct, c, :], op=ALU.mult), ["wend", "dtt"], ["wend"])
                S.dve(lambda e, c=c: e.tensor_tensor(out=xw[:ct, :].rearrange("p (h q) -> p h q", h=NH), in0=xtc[:ct, c, :].rearrange("p (h q) -> p h q", h=NH),
                                                     in1=wend[:ct, :].unsqueeze(2).broadcast_to([ct, NH, HP]), op=ALU.mult), ["xtc", "wend"], ["xw"])
                for g in range(NG):
                    pP = psr([1, 2]); pQ = 3 + (g % 2); pR = 5 + (g % 2)
                    gb = g % 2
                    for (Rb, Rk, dsrc, dk_) in ((Rhi[gb], f"Rhi{gb}", dthi, "dthi"), (Rlo[gb], f"Rlo{gb}", dtlo, "dtlo")):
                        S.pool(lambda e, Rb=Rb, dsrc=dsrc, c=c, g=g: e.tensor_tensor(out=Rb[:ct, :, 0:ct], in0=dsrc[:ct, c, g * 4:(g + 1) * 4].unsqueeze(2).broadcast_to([ct, 4, ct]),
                                                                              in1=LE_bf[:ct, 0:ct].unsqueeze(1).broadcast_to([ct, 4, ct]), op=ALU.mult), [dk_, "cbf"], [Rk])
                    for ri, (Rb, Rk) in enumerate(((Rhi[gb], f"Rhi{gb}"), (Rlo[gb], f"Rlo{gb}"))):
                        S.pe(lambda e, Rb=Rb, ri=ri, pP=pP: e.matmul(PS[pP][:ct, 0:4 * ct].rearrange("p (h t) -> p h t", h=4), lhsT=GT_bf[:ct, :ct], rhs=Rb[:ct, :, 0:ct],
                                                                      start=(ri == 0), stop=(ri == 1)), [Rk, "cbf"], [PK[pP]])
                    S.pe(lambda e, g=g, c=c, pQ=pQ: e.matmul(PS[pQ][:ct, 0:ct], lhsT=BT[:, g, c * ct:(c + 1) * ct], rhs=CT[:, g, c * ct:(c + 1) * ct], start=True, stop=True), ["BT", "CT"], [PK[pQ]])
                    S.act(lambda e, pP=pP: e.activation(out=E[:ct, :, 0:ct], in_=PS[pP][:ct, 0:4 * ct].rearrange("p (h t) -> p h t", h=4), func=AF.Exp), [PK[pP]], ["E"])
                    S.dve(lambda e, pQ=pQ: e.tensor_tensor(out=cbm[:ct, 0:ct], in0=PS[pQ][:ct, 0:ct], in1=LE_bf[:ct, 0:ct], op=ALU.mult), [PK[pQ], "cbf"], ["cbm"])
                    for h in range(4):
                        hh = g * 4 + h
                        S.dve(lambda e, h=h, hh=hh, c=c: e.scalar_tensor_tensor(out=mT[:ct, h, 0:ct], in0=E[:ct, h, 0:ct], scalar=dtt[:ct, c, hh:hh + 1], in1=cbm[:ct, 0:ct], op0=ALU.mult, op1=ALU.mult),
                              ["E", "dtt", "cbm"], ["mT"])
                    for h in range(4):
                        hh = g * 4 + h
                        S.pe(lambda e, h=h, hh=hh, c=c, pQ=pQ: e.matmul(PS[pQ][:ct, 128 + h * HP:128 + (h + 1) * HP], lhsT=mT[:ct, h, 0:ct], rhs=xtc[:ct, c, hh * HP:(hh + 1) * HP], start=True, stop=True),
                             ["mT", "xtc"], [PK[pQ]])
                    S.pe(lambda e, g=g, c=c, pR=pR: e.matmul(PS[pR][:ct, 0:256], lhsT=CT[:, g, c * ct:(c + 1) * ct], rhs=hT_bf[:, g * 256:(g + 1) * 256], start=True, stop=True), ["CT", "hT_bf"], [PK[pR]])
                    S.dve(lambda e, g=g, pR=pR: e.tensor_tensor(out=yo[:ct, :].rearrange("p (h q) -> p h q", h=4), in0=PS[pR][:ct, 0:256].rearrange("p (h q) -> p h q", h=4),
                                                              in1=ecum[:ct, g * 4:(g + 1) * 4].unsqueeze(2).broadcast_to([ct, 4, HP]), op=ALU.mult), [PK[pR], "ecum"], ["yo"])
                    S.dve(lambda e, pQ=pQ: e.tensor_tensor(out=y1[:ct, :], in0=PS[pQ][:ct, 128:384], in1=yo[:ct, :], op=ALU.add), [PK[pQ], "yo"], ["y1"])
                    S.pool(lambda e, g=g, c=c: e.tensor_tensor(out=yo[:ct, :].rearrange("p (h q) -> p h q", h=4), in0=xtc[:ct, c, g * 256:(g + 1) * 256].rearrange("p (h q) -> p h q", h=4),
                                                              in1=dsk[:ct, g * 4:(g + 1) * 4].unsqueeze(2).broadcast_to([ct, 4, HP]), op=ALU.mult), ["xtc", "dsk", "y1"], ["yo"])
                    S.dve(lambda e: e.tensor_tensor(out=y1[:ct, :], in0=y1[:ct, :], in1=yo[:ct, :], op=ALU.add), ["y1", "yo"], ["y1"])
                    S.dve(lambda e, g=g, c=c: e.tensor_tensor(out=y1[:ct, :], in0=y1[:ct, :], in1=sz[:ct, c, g * 256:(g + 1) * 256], op=ALU.mult), ["y1", "sz"], ["y1"])
                    S.pool(lambda e: e.tensor_tensor(out=junk[:ct, :], in0=y1[:ct, :], in1=y1[:ct, :], op=ALU.mult), ["y1"], ["junk"])
                    S.dve(lambda e: e.tensor_reduce(out=ss[:ct, :], in_=junk[:ct, :], axis=mybir.AxisListType.X, op=ALU.add), ["junk"], ["ss"])
                    S.act(lambda e: e.activation(out=ss[:ct, :], in_=ss[:ct, :], func=AF.Ln, bias=crms[:ct, :], scale=1.0 / 256.0), ["ss", "crms"], ["ss"])
                    S.act(lambda e: e.activation(out=ss[:ct, :], in_=ss[:ct, :], func=AF.Exp, scale=-0.5), ["ss"], ["ss"])
                    S.dve(lambda e: e.tensor_scalar(out=yn[:ct, :], in0=y1[:ct, :], scalar1=ss[:ct, :], scalar2=None, op0=ALU.mult), ["y1", "ss"], ["yn"])
                    pst = PS[0].bitcast(BF16)
                    for j in range(2):
                        S.pe(lambda e, j=j, pst=pst: e.transpose(out=pst[:, j * ct:(j + 1) * ct], in_=yn[:ct, j * 128:(j + 1) * 128], identity=ident_bf[:ct, :ct]), ["yn", "cbf"], [PK[0]])
                    for j in range(2):
                        S.act(lambda e, g=g, c=c, j=j, pst=pst: e.activation(out=yT[:, 2 * g + j, c * ct:(c + 1) * ct], in_=pst[:, j * ct:(j + 1) * ct], func=AF.Copy, scale=normg[:, 2 * g + j:2 * g + j + 1]), [PK[0], "normg"], ["xcT"])
                    S.pe(lambda e, g=g, c=c, pR=pR: e.matmul(PS[pR][:, 256:512], lhsT=Btok[:ct, c, g * 128:(g + 1) * 128], rhs=xw[:ct, g * 256:(g + 1) * 256], start=True, stop=True), ["Btok", "xw"], [PK[pR]])
                    S.pool(lambda e, g=g: e.tensor_tensor(out=hT[l][:, g * 256:(g + 1) * 256].rearrange("p (h q) -> p h q", h=4), in0=hT[l][:, g * 256:(g + 1) * 256].rearrange("p (h q) -> p h q", h=4),
                                                         in1=etot[:, g * 4:(g + 1) * 4].unsqueeze(2).broadcast_to([128, 4, HP]), op=ALU.mult), [hk, "etot", "hT_bf"], [hk])
                    S.dve(lambda e, g=g, pR=pR: e.tensor_tensor(out=hT[l][:, g * 256:(g + 1) * 256], in0=PS[pR][:, 256:512], in1=hT[l][:, g * 256:(g + 1) * 256], op=ALU.add), [PK[pR], hk], [hk])
                dsts = None
                if kind == "s":
                    dsts = ssm_s[l, c]
                elif last and c == nch - 1:
                    dsts = ssm_p[l]
                if dsts is not None:
                    for j in range(16):
                        pi = psr([3, 4])
                        S.pe(lambda e, j=j, pi=pi: e.transpose(out=PS[pi][:, 0:128], in_=hT[l][:, j * 128:(j + 1) * 128], identity=ident_f), [hk, "cf"], [PK[pi]])
                        S.act(lambda e, j=j, pi=pi: e.activation(out=stg[:, j, :], in_=PS[pi][:, 0:128], func=AF.Copy), [PK[pi]], ["stg"])
                    S.dma(dsts.rearrange("(j q) n -> q j n", q=128), stg, ["stg"], ["ssm_out"], "stg", q="pool")

            S.barrier()
            vb = alloc(F32, D)
            for i in range(4):
                wv, wk = w_next(f"wo{l}_{i}")
                for c in range(nch):
                    for hb in range(2):
                        pi = c * 2 + hb
                        for k in range(4):
                            S.pe(lambda e, c=c, hb=hb, k=k, i=i, wv=wv, pi=pi: e.matmul(PS[pi][:ct, :], lhsT=yT[:, i * 4 + k, c * ct:(c + 1) * ct], rhs=wv[:, k, hb * 512:(hb + 1) * 512],
                                                                                 start=(i == 0 and k == 0), stop=(i == 3 and k == 3)), [wk, "xcT"], [PK[pi]])
            for c in range(nch):
                layer_norm(ct, c, (c * 2, c * 2 + 1), lng, lnb, "lnp", x_tok[:ct, c, :], "x_tok", vb, "vb", alpha=ALPHA)
            S.barrier()

        def attn_mixer(l, kind, t0, ct, nch, tix):
            TT = ct * nch
            top[0] = PERSIST_TOP
            QT = alloc(BF16, 4, 512)
            oT = alloc(BF16, 4, 512)
            KS = [alloc(BF16, 4, 512) for _ in range(2)]
            VS = [alloc(BF16, 4, SBW) for _ in range(2)]
            eb = [alloc(F32, 512) for _ in range(2)]
            spb = [alloc(F32, 512) for _ in range(2)]
            spbf = [alloc(BF16, 512) for _ in range(2)]
            tb = [alloc(F32, 512) for _ in range(2)]
            wTb = [alloc(BF16, 512) for _ in range(2)]
            Aac = [alloc(F32, 512) for _ in range(2)]
            Ahi = [alloc(BF16, 512) for _ in range(2)]
            Alo = [alloc(BF16, 512) for _ in range(2)]
            vb = alloc(F32, D)
            DBG.update(dict(QT=QT, oT=oT, KS0=KS[0], KS1=KS[1], VS0=VS[0], VS1=VS[1], eb0=eb[0], sp0=spb[0], tb0=tb[0], wT0=wTb[0], eb1=eb[1], sp1=spb[1], tb1=tb[1], wT1=wTb[1], A0=Aac[0], A1=Aac[1], xT=xT))
            load_ln(ln1_g[l], ln1_b[l])
            scale = float(128 ** -0.5)
            wv, wk = w_next(f"q{l}")
            for h in range(4):
                pi = psr([6, 7])
                for kc in range(8):
                    S.pe(lambda e, kc=kc, h=h, wv=wv, pi=pi: e.matmul(PS[pi][:, 0:TT], lhsT=wv[:, kc, h * 128:(h + 1) * 128], rhs=xT[:, kc, 0:TT], start=(kc == 0), stop=(kc == 7)), [wk, "xT"], [PK[pi]])
                S.act(lambda e, h=h, pi=pi: e.activation(out=QT[:, h, 0:TT], in_=PS[pi][:, 0:TT], func=AF.Copy, scale=scale), [PK[pi]], ["QT"])
            if kind == "p":
                nkb = (t0 + TT) // 128
                units = [(kT_p, vv_p, "kT_p", "vv_p", nkb, 0, TT, None)]
            else:
                units = [(kT_s[s_], vv_s[s_], f"kT_s{s_}", f"vv_s{s_}", PAST // 128 + 1, s_ * ct, ct, ct) for s_ in range(nch)]
            if os.environ.get("REV_UNITS"):
                units = units[::-1]
            def do_unit(kTd, vd, kTk, vk, nkb, q0, nq, lastn):
                ngr = (nkb + 3) // 4
                for hp in range(2):
                    first = True
                    for hi in range(2):
                        S.pool(lambda e, hi=hi: e.memset(Aac[hi], 0.0), [], [f"A{hi}"])
                    for gi in range(ngr - 1, -1, -1):
                        sl = psr([0, 1])
                        kb0 = gi * 4
                        nb_here = min(4, nkb - kb0)
                        nkeys = nb_here * 128
                        S.dma(KS[sl][:, :, 0:nkeys], kTd[:, :, kb0 * 128:kb0 * 128 + nkeys], [kTk], [f"KS{sl}"], f"KS{sl}")
                        if nkeys % 128 == 0:
                            S.dma(VS[sl][:, 0:nb_here, :], vd[kb0 * 128:kb0 * 128 + nkeys, :].rearrange("(b p) c -> p b c", p=128), [vk], [f"VS{sl}"], f"VS{sl}")
                        else:
                            assert nb_here == 1
                            S.dma(VS[sl][:nkeys, 0, :], vd[kb0 * 128:kb0 * 128 + nkeys, :], [vk], [f"VS{sl}"], f"VS{sl}")
                        for bi in range(nb_here - 1, -1, -1):
                            kb = kb0 + bi
                            nk = 128
                            mask = None
                            if kind == "p":
                                kl = kb - t0 // 128
                                if kl >= 0:
                                    mask = M4[:, kl, 0:nq]
                            else:
                                if kb == nkb - 1:
                                    mask = M4[:, 0, 0:nq]
                            for hi in range(2):
                                h = hp * 2 + hi
                                pz = psr([6, 7]); pB = 2 + hi; pO = 4 + hi
                                b2 = psr([0, 1])
                                e_ = eb[b2]; sp_ = spb[b2]; spf = spbf[b2]; t_ = tb[b2]; w_ = wTb[b2]
                                S.pe(lambda e, h=h, bi=bi, nk=nk, pz=pz, sl=sl: e.matmul(PS[pz][:nk, 0:nq], lhsT=KS[sl][:, h, bi * 128:bi * 128 + nk], rhs=QT[:, h, q0:q0 + nq], start=True, stop=True),
                                     [f"KS{sl}", "QT"], [PK[pz]])
                                S.act(lambda e, e_=e_, nk=nk, pz=pz: e.activation(out=e_[:nk, 0:nq], in_=PS[pz][:nk, 0:nq], func=AF.Exp), [PK[pz]], [f"eb{b2}"])
                                S.act(lambda e, e_=e_, sp_=sp_, nk=nk: e.activation(out=sp_[:nk, 0:nq], in_=e_[:nk, 0:nq], func=AF.Ln, bias=cone[:nk, :], scale=1.0), [f"eb{b2}", "cone"], [f"sp{b2}"])
                                if mask is not None:
                                    S.dve(lambda e, sp_=sp_, nk=nk, mask=mask: e.tensor_tensor(out=sp_[:nk, 0:nq], in0=sp_[:nk, 0:nq], in1=mask, op=ALU.mult), [f"sp{b2}", "cbf"], [f"sp{b2}"])
                                S.pool(lambda e, sp_=sp_, spf=spf, nk=nk: e.tensor_copy(out=spf[:nk, 0:nq], in_=sp_[:nk, 0:nq]), [f"sp{b2}"], [f"spf{b2}"])
                                S.pe(lambda e, spf=spf, nk=nk, pB=pB, first=first: e.matmul(PS[pB][:, 0:nq], lhsT=GT_bf[:nk, :], rhs=spf[:nk, 0:nq], start=True, stop=first), [f"spf{b2}", "cbf"], [PK[pB]])
                                if not first:
                                    S.pe(lambda e, pB=pB, hi=hi: e.matmul(PS[pB][:, 0:nq], lhsT=ones_bf, rhs=Ahi[hi][:, 0:nq], start=False, stop=False), [f"Ahi{hi}", "cbf"], [PK[pB]])
                                    S.pe(lambda e, pB=pB, hi=hi: e.matmul(PS[pB][:, 0:nq], lhsT=ones_bf, rhs=Alo[hi][:, 0:nq], start=False, stop=True), [f"Alo{hi}", "cbf"], [PK[pB]])
                                S.dve(lambda e, t_=t_, sp_=sp_, nk=nk, pz=pz: e.tensor_tensor(out=t_[:nk, 0:nq], in0=PS[pz][:nk, 0:nq], in1=sp_[:nk, 0:nq], op=ALU.subtract), [PK[pz], f"sp{b2}"], [f"tb{b2}"])
                                S.dve(lambda e, t_=t_, nk=nk, pB=pB: e.tensor_tensor(out=t_[:nk, 0:nq], in0=t_[:nk, 0:nq], in1=PS[pB][:nk, 0:nq], op=ALU.subtract), [f"tb{b2}", PK[pB]], [f"tb{b2}"])
                                S.pool(lambda e, sp_=sp_, nk=nk, hi=hi: e.tensor_tensor(out=Aac[hi][:nk, 0:nq], in0=Aac[hi][:nk, 0:nq], in1=sp_[:nk, 0:nq], op=ALU.add), [f"A{hi}", f"sp{b2}"], [f"A{hi}"])
                                S.dve(lambda e, hi=hi: e.tensor_copy(out=Ahi[hi][:, 0:nq], in_=Aac[hi][:, 0:nq]), [f"A{hi}"], [f"Ahi{hi}"])
                                S.dve(lambda e, hi=hi: e.tensor_tensor(out=Alo[hi][:, 0:nq], in0=Aac[hi][:, 0:nq], in1=Ahi[hi][:, 0:nq], op=ALU.subtract), [f"A{hi}", f"Ahi{hi}"], [f"Alo{hi}"])
                                S.act(lambda e, t_=t_, w_=w_, nk=nk: e.activation(out=w_[:nk, 0:nq], in_=t_[:nk, 0:nq], func=AF.Exp), [f"tb{b2}"], [f"wT{b2}"])
                                if mask is not None:
                                    S.pool(lambda e, w_=w_, nk=nk, mask=mask: e.tensor_tensor(out=w_[:nk, 0:nq], in0=w_[:nk, 0:nq], in1=mask, op=ALU.mult), [f"wT{b2}", "cbf"], [f"wT{b2}"])
                                if kind == "s" and os.environ.get("ATT_CUT"):
                                    DBG["cnt"] = DBG.get("cnt", 0) + 1
                                    if DBG["cnt"] >= int(os.environ["ATT_CUT"]):
                                        DBG["info"] = dict(pz=pz, pB=pB, pO=pO, b2=b2, h=h, nk=nk, nq=nq, q0=q0, sl=sl, bi=bi, kb=kb)
                                        raise StopBuild()
                                lastblk = (gi == 0 and bi == 0)
                                S.pe(lambda e, w_=w_, nk=nk, pO=pO, h=h, bi=bi, sl=sl, first=first, lastblk=lastblk: e.matmul(PS[pO][:, 0:nq], lhsT=VS[sl][:nk, bi, h * 128:(h + 1) * 128], rhs=w_[:nk, 0:nq], start=first, stop=lastblk),
                                     [f"wT{b2}", f"VS{sl}"], [PK[pO]])
                            first = False
                    for hi in range(2):
                        h = hp * 2 + hi
                        S.act(lambda e, h=h, hi=hi: e.activation(out=oT[:, h, q0:q0 + nq], in_=PS[4 + hi][:, 0:nq], func=AF.Copy), [PK[4 + hi]], ["oT"])
            for u in units:
                do_unit(*u)
            S.barrier()
            wv, wk = w_next(f"bo{l}")
            for c in range(nch):
                for hb in range(2):
                    pi = c * 2 + hb
                    for h in range(4):
                        S.pe(lambda e, c=c, hb=hb, h=h, wv=wv, pi=pi: e.matmul(PS[pi][:ct, :], lhsT=oT[:, h, c * ct:(c + 1) * ct], rhs=wv[:, h, hb * 512:(hb + 1) * 512], start=(h == 0), stop=(h == 3)),
                             [wk, "oT"], [PK[pi]])
            for c in range(nch):
                layer_norm(ct, c, (c * 2, c * 2 + 1), lng, lnb, "lnp", x_tok[:ct, c, :], "x_tok", vb, "vb", alpha=ALPHA)
            S.barrier()

        def mlp_ple(l, kind, t0, ct, nch, tix):
            TT = ct * nch
            top[0] = PERSIST_TOP
            hTm = alloc(BF16, 32, 512)
            rl = [alloc(F32, 512) for _ in range(2)]
            vb = alloc(F32, D)
            xbf = alloc(BF16, D)
            ptok = alloc(F32, 4, PLE); pbf = alloc(BF16, PLE); pT = alloc(BF16, 2, 512)
            sg = alloc(F32, 512); tg = alloc(F32, 512)
            load_ln(ln2_g[l], ln2_b[l])
            if kind == "p":
                S.dma(ptok[:ct, 0:nch, :], p_p[l][t0:t0 + TT, :].rearrange("(c p) d -> p c d", p=ct), [], ["ptok"], "ptok")
            else:
                S.dma(ptok[:ct, 0:nch, :], p_s[l].rearrange("c p d -> p c d"), [], ["ptok"], "ptok")
            transpose_to_xT(ct, nch, x_tok, "x_tok", xT, "xT", 8, xbf, "xbf")
            for i in range(8):
                wv, wk = w_next(f"w1{l}_{i}")
                for j in range(4):
                    oc = i * 4 + j
                    pi = psr([1, 2, 3, 4])
                    for kc in range(8):
                        S.pe(lambda e, kc=kc, j=j, wv=wv, pi=pi: e.matmul(PS[pi][:, 0:TT], lhsT=wv[:, kc, j * 128:(j + 1) * 128], rhs=xT[:, kc, 0:TT], start=(kc == 0), stop=(kc == 7)), [wk, "xT"], [PK[pi]])
                    b = oc % 2
                    S.act(lambda e, b=b, pi=pi: e.activation(out=rl[b][:, 0:TT], in_=PS[pi][:, 0:TT], func=AF.Relu), [PK[pi]], [f"rl{b}"])
                    eng = S.pool if oc % 2 == 0 else S.dve
                    eng(lambda e, b=b, oc=oc: e.tensor_tensor(out=hTm[:, oc, 0:TT], in0=rl[b][:, 0:TT], in1=rl[b][:, 0:TT], op=ALU.mult), [f"rl{b}"], ["hTm"])
            S.barrier()
            for i in range(8):
                wv, wk = w_next(f"w2{l}_{i}")
                for c in range(nch):
                    for hb in range(2):
                        pi = c * 2 + hb
                        for k in range(4):
                            S.pe(lambda e, c=c, hb=hb, k=k, i=i, wv=wv, pi=pi: e.matmul(PS[pi][:ct, :], lhsT=hTm[:, i * 4 + k, c * ct:(c + 1) * ct], rhs=wv[:, k, hb * 512:(hb + 1) * 512],
                                                                                 start=(i == 0 and k == 0), stop=(i == 7 and k == 3)), [wk, "hTm"], [PK[pi]])
            for c in range(nch):
                layer_norm(ct, c, (c * 2, c * 2 + 1), lng, lnb, "lnp", x_tok[:ct, c, :], "x_tok", vb, "vb", alpha=ALPHA)
            S.barrier()
            transpose_to_xT(ct, nch, x_tok, "x_tok", xT, "xT", 8, xbf, "xbf")
            transpose_to_xT(ct, nch, ptok, "ptok", pT, "pT", 2, pbf, "pbf")
            gws = [w_next(f"g{l}_0"), w_next(f"g{l}_1", back=1)]
            pwv, pwk = w_next(f"pw{l}", back=2)
            for c in range(nch):
                for hb in range(2):
                    wv, wk = gws[hb]
                    pg = psr([1, 2]); pp = psr([3, 4])
                    for kc in range(8):
                        S.pe(lambda e, kc=kc, c=c, wv=wv, pg=pg: e.matmul(PS[pg][:ct, :], lhsT=xT[:, kc, c * ct:(c + 1) * ct], rhs=wv[:, kc, :], start=(kc == 0), stop=(kc == 7)), [wk, "xT"], [PK[pg]])
                    for k in range(2):
                        S.pe(lambda e, k=k, c=c, hb=hb, pp=pp: e.matmul(PS[pp][:ct, :], lhsT=pT[:, k, c * ct:(c + 1) * ct], rhs=pwv[:, k, hb * 512:(hb + 1) * 512], start=(k == 0), stop=(k == 1)), [pwk, "pT"], [PK[pp]])
                    S.act(lambda e, pg=pg: e.activation(out=sg[:ct, :], in_=PS[pg][:ct, :], func=AF.Sigmoid), [PK[pg]], ["sg"])
                    S.dve(lambda e, pp=pp: e.tensor_tensor(out=tg[:ct, :], in0=PS[pp][:ct, :], in1=sg[:ct, :], op=ALU.mult), [PK[pp], "sg"], ["tg"])
                    S.pool(lambda e, c=c, hb=hb: e.tensor_tensor(out=x_tok[:ct, c, hb * 512:(hb + 1) * 512], in0=x_tok[:ct, c, hb * 512:(hb + 1) * 512], in1=tg[:ct, :], op=ALU.add), ["x_tok", "tg"], ["x_tok"])
            S.barrier()

        def kv_stage(kind, t0, ct, nch, tix):
            TT = ct * nch
            top[0] = PERSIST_TOP
            kvn = alloc(F32, 4, D); vb = alloc(F32, D)
            kvbf = alloc(BF16, D); kvT = alloc(BF16, 8, 512)
            kg = alloc(F32, D); kb_ = alloc(F32, D)
            ktk = alloc(F32, 4, SBW); vtk = alloc(F32, 4, SBW); vbf = alloc(BF16, 4, SBW)
            kTt = alloc(BF16, 4, 512)
            S.dma(kg, kv_g.partition_broadcast(128), [], ["kgb"], "kgb")
            S.dma(kb_, kv_b.partition_broadcast(128), [], ["kgb"], "kgb")
            for c in range(nch):
                layer_norm(ct, c, None, kg, kb_, "kgb", kvn[:ct, c, :], "kvn", vb, "vb", src=x_tok[:ct, c, :], src_key="x_tok")
            KVC = int(os.environ.get("KV_CUT", "9"))
            if KVC <= 0:
                S.barrier(); return
            transpose_to_xT(ct, nch, kvn, "kvn", kvT, "kvT", 8, kvbf, "kvbf")
            wks = [w_next("kv_0"), w_next("kv_1", back=1)]
            if KVC <= 1:
                S.barrier(); return
            for c in range(nch):
                for hb in range(2):
                    wv, wk = wks[hb]
                    pi = psr([1, 2])
                    for kc in range(8):
                        S.pe(lambda e, kc=kc, c=c, wv=wv, pi=pi: e.matmul(PS[pi][:ct, :], lhsT=kvT[:, kc, c * ct:(c + 1) * ct], rhs=wv[:, kc, :], start=(kc == 0), stop=(kc == 7)), [wk, "kvT"], [PK[pi]])
                    dst = ktk if hb == 0 else vtk
                    dk = "ktk" if hb == 0 else "vtk"
                    S.act(lambda e, dst=dst, c=c, pi=pi: e.activation(out=dst[:ct, c, :], in_=PS[pi][:ct, :], func=AF.Copy), [PK[pi]], [dk])
                    if hb == 1:
                        S.dve(lambda e, c=c, pi=pi: e.tensor_copy(out=vbf[:ct, c, :], in_=PS[pi][:ct, :]), [PK[pi]], ["vbf"])
            if KVC <= 2:
                S.barrier(); return
            wv, wk = wks[0]
            for h in range(4):
                pi = psr([3, 4])
                for kc in range(8):
                    S.pe(lambda e, kc=kc, h=h, wv=wv, pi=pi: e.matmul(PS[pi][:, 0:TT], lhsT=wv[:, kc, h * 128:(h + 1) * 128], rhs=kvT[:, kc, 0:TT], start=(kc == 0), stop=(kc == 7)), [wk, "kvT"], [PK[pi]])
                S.act(lambda e, h=h, pi=pi: e.activation(out=kTt[:, h, 0:TT], in_=PS[pi][:, 0:TT], func=AF.Copy), [PK[pi]], ["kTt"])
            if KVC <= 3:
                S.barrier(); return
            if kind == "p":
                S.dma(k_p[t0:t0 + TT, :].rearrange("(c p) d -> p c d", p=ct), ktk[:ct, 0:nch, :], ["ktk"], ["k_out"], "ktk", q="pool")
                S.dma(v_p[t0:t0 + TT, :].rearrange("(c p) d -> p c d", p=ct), vtk[:ct, 0:nch, :], ["vtk"], ["v_out"], "vtk", q="pool")
                S.dma(vv_p[t0:t0 + TT, :].rearrange("(c p) d -> p c d", p=ct), vbf[:ct, 0:nch, :], ["vbf"], ["vv_p"], "vbf", q="pool")
                S.dma(kT_p[:, :, t0:t0 + TT], kTt[:, :, 0:TT], ["kTt"], ["kT_p"], "kTt", q="pool")
            else:
                S.dma(k_s.rearrange("c p d -> p c d"), ktk[:ct, 0:nch, :], ["ktk"], ["k_out"], "ktk", q="pool")
                S.dma(v_s.rearrange("c p d -> p c d"), vtk[:ct, 0:nch, :], ["vtk"], ["v_out"], "vtk", q="pool")
                for s_ in range(nch):
                    S.dma(vv_s[s_][PAST:PAST + ct, :], vbf[:ct, s_, :], ["vbf"], [f"vv_s{s_}"], f"vbf{s_}", q="pool")
                    S.dma(kT_s[s_][:, :, PAST:PAST + ct], kTt[:, :, s_ * ct:(s_ + 1) * ct], ["kTt"], [f"kT_s{s_}"], f"kTt{s_}", q="pool")
            S.barrier()

        try:
            stage("prologue")
            for tix, (kind, t0, ct, nch) in enumerate(tiles):
                TT = ct * nch
                if kind == "p":
                    S.dma(x_tok[:ct, 0:nch, :], x_p[t0:t0 + TT, :].rearrange("(c p) d -> p c d", p=ct), [], ["x_tok"], "x_tok")
                else:
                    S.dma(x_tok[:ct, 0:nch, :], x_s.rearrange("c p d -> p c d"), [], ["x_tok"], "x_tok")
                for l in range(DEPTH):
                    top[0] = PERSIST_TOP
                    xbf0 = alloc(BF16, D)
                    transpose_to_xT(ct, nch, x_tok, "x_tok", xT, "xT", 8, xbf0, "xbf0")
                    S.barrier()
                    stage(f"xT{l}")
                    if l < 2:
                        mamba_mixer(l, kind, t0, ct, nch, tix)
                    else:
                        attn_mixer(l, kind, t0, ct, nch, tix)
                    stage(f"mixer{l}")
                    mlp_ple(l, kind, t0, ct, nch, tix)
                    stage(f"mlp{l}")
                    if l == 1:
                        kv_stage(kind, t0, ct, nch, tix)
                        stage("kv")
                if kind == "p":
                    S.dma(y_p[t0:t0 + TT, :].rearrange("(c p) d -> p c d", p=ct), x_tok[:ct, 0:nch, :], ["x_tok"], ["y_out"], "x_tok", q="pool")
                else:
                    S.dma(y_s.rearrange("c p d -> p c d"), x_tok[:ct, 0:nch, :], ["x_tok"], ["y_out"], "x_tok", q="pool")
            assert wstate["next_use"] == len(worder)
        except StopBuild:
            if kind == "p":
                S.dma(y_p[t0:t0 + TT, :].rearrange("(c p) d -> p c d", p=ct), x_tok[:ct, 0:nch, :], ["x_tok"], ["y_out"], "x_tok", q="pool")
            else:
                S.dma(y_s.rearrange("c p d -> p c d"), x_tok[:ct, 0:nch, :], ["x_tok"], ["y_out"], "x_tok", q="pool")
        S.finish()
    return nc, S


def make_consts():
    a = np.arange(128)
    ident = (a[:, None] == a[None, :]).astype(np.float32)
    GT = (a[:, None] > a[None, :]).astype(np.float32)
    LE = (a[:, None] <= a[None, :]).astype(np.float32)
    LT = (a[:, None] < a[None, :]).astype(np.float32)
    ones = np.ones((128, 128), np.float32)
    M4 = np.zeros((128, 4, 512), np.float32)
    for kl in range(4):
        M4[:, kl, kl * 128:(kl + 1) * 128] = LT
        M4[:, kl, (kl + 1) * 128:] = 1.0
    cbf = np.concatenate([ident, GT, LE, ones, M4.reshape(128, 2048)], axis=1).astype(ml_dtypes.bfloat16)
    cf = np.concatenate([ident, LE, ones], axis=1).astype(np.float32)
    return cbf, cf


_CACHE = {}


def run(inputs, SEQ, n_cores=8, KSTOP=None):
    if SEQ not in _CACHE:
        _CACHE[SEQ] = build(SEQ, KSTOP=KSTOP)
    nc, S = _CACHE[SEQ]
    cbf, cf = make_consts()
    f = lambda a: np.ascontiguousarray(np.asarray(a, dtype=np.float32))
    wnames = ["a_w_in", "a_conv_w", "a_conv_b", "a_dt_bias", "a_A_log", "a_d_skip", "a_norm_g", "a_w_out", "kv_norm_g", "kv_norm_b",
              "w_kv", "b_w_q", "b_w_out", "ln1_g", "ln1_b", "ln2_g", "ln2_b", "mlp_w1", "mlp_w2", "ple_w", "ple_gate_w"]
    shared = {n: f(inputs[n]) for n in wnames}
    shared["cst_bf"] = cbf; shared["cst_f"] = cf
    in_maps = []
    for b in range(n_cores):
        m = dict(shared)
        m["x_p"] = f(inputs["x_prompt"][b]); m["x_s"] = f(inputs["x_sample"][2 * b:2 * b + 2])
        m["p_p"] = f(inputs["p_prompt"][:, b]); m["p_s"] = f(inputs["p_sample"][:, 2 * b:2 * b + 2])
        m["st_ssm"] = f(inputs["state_ssm"][:, 2 * b:2 * b + 2]).reshape(2, 2, NH * HP, NS)
        m["st_conv"] = f(inputs["state_conv"][:, 2 * b:2 * b + 2])
        m["c_k"] = f(inputs["cache_k"][2 * b:2 * b + 2]).reshape(2, PAST, SBW)
        m["c_v"] = f(inputs["cache_v"][2 * b:2 * b + 2]).reshape(2, PAST, SBW)
        in_maps.append(m)
    res = run_bass_kernel_spmd(nc, in_maps, core_ids=list(range(n_cores)))
    R = res.results
    B = n_cores
    y_prompt = np.stack([R[b]["y_p"] for b in range(B)])
    y_sample = np.concatenate([R[b]["y_s"] for b in range(B)], axis=0)
    ssm_p = np.stack([R[b]["ssm_p"].reshape(2, NH, HP, NS) for b in range(B)], axis=1)
    conv_p = np.stack([R[b]["conv_p"] for b in range(B)], axis=1)
    k_p = np.stack([R[b]["k_p"].reshape(SEQ, 4, 128) for b in range(B)])
    v_p = np.stack([R[b]["v_p"].reshape(SEQ, 4, 128) for b in range(B)])
    ssm_s = np.concatenate([R[b]["ssm_s"].reshape(2, 2, NH, HP, NS) for b in range(B)], axis=1)
    conv_s = np.concatenate([R[b]["conv_s"] for b in range(B)], axis=1)
    k_s = np.concatenate([R[b]["k_s"].reshape(2, 16, 4, 128) for b in range(B)], axis=0)
    v_s = np.concatenate([R[b]["v_s"].reshape(2, 16, 4, 128) for b in range(B)], axis=0)
    outs = (y_prompt, y_sample, ssm_p, conv_p, k_p, v_p, ssm_s, conv_s, k_s, v_s)
    return tuple(np.ascontiguousarray(o.astype(np.float32)) for o in outs)


def kernel(**inputs):
    return run(inputs, int(np.asarray(inputs["x_prompt"]).shape[1]))
```

```python
from contextlib import ExitStack
import os
import numpy as np
import ml_dtypes
import concourse.bass as bass
import concourse.mybir as mybir
from concourse.bass_utils import run_bass_kernel_spmd

F32 = mybir.dt.float32
BF16 = mybir.dt.bfloat16
AF = mybir.ActivationFunctionType
ALU = mybir.AluOpType

D = 1024
DI = 2048
NH = 32
HP = 64
NG = 8
NS = 128
CONVD = 4096
DINP = 6176
DFF = 4096
PLE = 256
SBW = 512
PAST = 1024
DEPTH = 4
ALPHA = (2.0 * DEPTH) ** 0.25
LN_EPS = 1e-5
RMS_EPS = 1e-5
SEM_ROT = 30000
PIECE = 4096
NSLOT = 4


class Op:
    __slots__ = ("eng", "fn", "dma", "deps", "signal", "sig", "idx", "dkey", "persist")

    def __init__(self, eng, fn, dma, dkey, persist):
        self.eng = eng; self.fn = fn; self.dma = dma; self.deps = []; self.signal = False
        self.sig = None; self.dkey = dkey; self.persist = persist


class Sched:
    def __init__(self, nc):
        self.nc = nc
        self.ops = []
        self.last_w = {}
        self.readers = {}
        self.last_eng = {}
        self.dma_since_bar = []
        self.pending = {}
        self.bar_deps = []

    def op(self, eng, fn, reads=(), writes=(), dma=False, dkey=None, persist=False):
        o = Op(eng, fn, dma, dkey, persist)
        o.idx = len(self.ops)
        deps = {}
        for k in reads:
            w = self.last_w.get(k)
            if w is not None:
                deps[w.idx] = (w, True)
            if k.startswith("ps"):
                for r in self.readers.get(k, ()):
                    if r.eng != eng and r.idx not in deps:
                        deps[r.idx] = (r, True)
        for k in writes:
            w = self.last_w.get(k)
            if w is not None and w.idx not in deps:
                deps[w.idx] = (w, False)
            for r in self.readers.get(k, ()):
                if r.idx not in deps:
                    deps[r.idx] = (r, False)
        for (d, raw) in deps.values():
            if d.eng == eng and not dma and not d.dma:
                if not raw or eng == "pe":
                    continue
            o.deps.append(d)
            d.signal = True
        if not dma and eng in self.pending:
            for d in self.pending.pop(eng):
                if d.dma or d.eng != eng:
                    o.deps.append(d)
        for k in writes:
            self.last_w[k] = o
            self.readers[k] = []
        for k in reads:
            if k not in writes:
                self.readers.setdefault(k, []).append(o)
        self.ops.append(o)
        if dma:
            o.signal = True
            if not persist:
                self.dma_since_bar.append(o)
                for d in self.bar_deps:
                    if d.dma or d.eng != eng:
                        o.deps.append(d)
        else:
            self.last_eng[eng] = o
        return o

    def pe(self, fn, reads=(), writes=()): return self.op("pe", fn, reads, writes)
    def act(self, fn, reads=(), writes=()): return self.op("act", fn, reads, writes)
    def dve(self, fn, reads=(), writes=()): return self.op("dve", fn, reads, writes)
    def pool(self, fn, reads=(), writes=()): return self.op("pool", fn, reads, writes)

    def dma(self, out, in_, reads, writes, dkey, q="sp", persist=False, **kw):
        return self.op(q, lambda e: e.dma_start(out=out, in_=in_, **kw), reads, writes, dma=True,
                       dkey=dkey + ("_sw" if q == "pool" else ""), persist=persist)

    def barrier(self):
        deps = list(self.last_eng.values()) + list(self.dma_since_bar)
        self.dma_since_bar = []
        for d in deps:
            d.signal = True
        self.pending = {eng: deps for eng in ("pe", "act", "dve", "pool")}
        self.bar_deps = deps

    def finish(self):
        nc = self.nc
        ops = self.ops
        sem_names = {}
        eng_cnt = {}
        dk_cnt = {}
        per = SEM_ROT // 16
        for o in ops:
            if not o.signal:
                continue
            if o.dma:
                c = dk_cnt.get(o.dkey, 0) + 1
                dk_cnt[o.dkey] = c
                rot = (c - 1) // per
                o.sig = (f"d_{o.dkey}_{rot}", 16 * (c - rot * per))
            else:
                c = eng_cnt.get(o.eng, 0) + 1
                eng_cnt[o.eng] = c
                rot = (c - 1) // SEM_ROT
                o.sig = (f"e_{o.eng}_{rot}", c - rot * SEM_ROT)
            sem_names[o.sig[0]] = None
        last_dma_sig = {}
        for o in ops:
            if o.dma:
                last_dma_sig[o.sig[0]] = max(last_dma_sig.get(o.sig[0], 0), o.sig[1])
        by_eng = {}
        for o in ops:
            by_eng.setdefault(o.eng, []).append(o)
        self.n_ops = len(ops)
        with ExitStack() as es:
            sems = {n: es.enter_context(nc.semaphore(n)) for n in sem_names}
            block = es.enter_context(nc.Block())

            def emit(engname):
                def body(e):
                    waited = {}
                    for o in by_eng.get(engname, ()):
                        need = {}
                        for d in o.deps:
                            sn, sv = d.sig
                            if sv > need.get(sn, 0):
                                need[sn] = sv
                        for sn, sv in need.items():
                            if waited.get(sn, 0) >= sv:
                                continue
                            e.wait_ge(sems[sn], sv)
                            waited[sn] = sv
                        ins = o.fn(e)
                        if o.signal:
                            ins.then_inc(sems[o.sig[0]], 16 if o.dma else 1)
                    if engname == "sp":
                        for sn, sv in last_dma_sig.items():
                            if waited.get(sn, 0) < sv:
                                e.wait_ge(sems[sn], sv)
                return body

            block.sync(emit("sp"))
            block.scalar(emit("act"))
            block.vector(emit("dve"))
            block.gpsimd(emit("pool"))
            block.tensor(emit("pe"))


DBG = {}


class StopBuild(Exception):
    pass


def build(SEQ, NSEQ_S=2, TS=16, KSTOP=None):
    nc = bass.Bass("TRN2", target_bir_lowering=False)
    stg_ = {"i": 0}

    def stage(name):
        stg_["i"] += 1
        if KSTOP is not None and stg_["i"] >= KSTOP:
            print("STOP at stage", stg_["i"], name)
            raise StopBuild()

    S = Sched(nc)

    def din(name, shape, dt=F32):
        return nc.dram_tensor(name, list(shape), dt, kind="ExternalInput").ap()

    def dout(name, shape, dt=F32):
        return nc.dram_tensor(name, list(shape), dt, kind="ExternalOutput").ap()

    x_p = din("x_p", [SEQ, D]); x_s = din("x_s", [NSEQ_S, TS, D])
    p_p = din("p_p", [DEPTH, SEQ, PLE]); p_s = din("p_s", [DEPTH, NSEQ_S, TS, PLE])
    st_ssm = din("st_ssm", [2, NSEQ_S, NH * HP, NS]); st_conv = din("st_conv", [2, NSEQ_S, 3, CONVD])
    c_k = din("c_k", [NSEQ_S, PAST, SBW]); c_v = din("c_v", [NSEQ_S, PAST, SBW])
    a_w_in = din("a_w_in", [2, D, DINP]); a_conv_w = din("a_conv_w", [2, 4, CONVD]); a_conv_b = din("a_conv_b", [2, CONVD])
    a_dt_bias = din("a_dt_bias", [2, NH]); a_A_log = din("a_A_log", [2, NH]); a_d_skip = din("a_d_skip", [2, NH])
    a_norm_g = din("a_norm_g", [2, DI]); a_w_out = din("a_w_out", [2, DI, D])
    kv_g = din("kv_norm_g", [D]); kv_b = din("kv_norm_b", [D]); w_kv = din("w_kv", [D, 2 * SBW])
    b_w_q = din("b_w_q", [2, D, SBW]); b_w_out = din("b_w_out", [2, SBW, D])
    ln1_g = din("ln1_g", [DEPTH, D]); ln1_b = din("ln1_b", [DEPTH, D]); ln2_g = din("ln2_g", [DEPTH, D]); ln2_b = din("ln2_b", [DEPTH, D])
    mlp_w1 = din("mlp_w1", [DEPTH, D, DFF]); mlp_w2 = din("mlp_w2", [DEPTH, DFF, D])
    ple_w = din("ple_w", [DEPTH, PLE, D]); gate_w = din("ple_gate_w", [DEPTH, D, D])
    NCB = 512 + 2048
    cst_bf_d = din("cst_bf", [128, NCB], BF16); cst_f_d = din("cst_f", [128, 384])

    y_p = dout("y_p", [SEQ, D]); y_s = dout("y_s", [NSEQ_S, TS, D])
    ssm_p = dout("ssm_p", [2, NH * HP, NS]); conv_p = dout("conv_p", [2, 3, CONVD])
    k_p = dout("k_p", [SEQ, SBW]); v_p = dout("v_p", [SEQ, SBW])
    ssm_s = dout("ssm_s", [2, NSEQ_S, NH * HP, NS]); conv_s = dout("conv_s", [2, NSEQ_S, 3, CONVD])
    k_s = dout("k_s", [NSEQ_S, TS, SBW]); v_s = dout("v_s", [NSEQ_S, TS, SBW])

    pieces = {}
    plist = []

    def add_piece(name, src, a, b):
        pieces[name] = (len(plist), a, b)
        plist.append((name, src, a, b))

    def colblock(W, n0, nw):
        return W[:, n0:n0 + nw].rearrange("(k p) n -> p k n", p=128)

    def rowblock(W, k0, kk):
        return W[k0 * 128:(k0 + kk) * 128, :].rearrange("(k p) n -> p k n", p=128)

    for l in range(DEPTH):
        if l < 2:
            for i in range(8):
                add_piece(f"xbc{l}_{i}", colblock(a_w_in[l], DI + i * 512, 512), 8, 512)
            for i in range(4):
                add_piece(f"z{l}_{i}", colblock(a_w_in[l], i * 512, 512), 8, 512)
            add_piece(f"dt{l}", colblock(a_w_in[l], DI + CONVD, NH), 8, NH)
            for i in range(4):
                add_piece(f"wo{l}_{i}", rowblock(a_w_out[l], i * 4, 4), 4, D)
        else:
            add_piece(f"q{l}", colblock(b_w_q[l - 2], 0, SBW), 8, SBW)
            add_piece(f"bo{l}", rowblock(b_w_out[l - 2], 0, 4), 4, D)
        for i in range(8):
            add_piece(f"w1{l}_{i}", colblock(mlp_w1[l], i * 512, 512), 8, 512)
        for i in range(8):
            add_piece(f"w2{l}_{i}", rowblock(mlp_w2[l], i * 4, 4), 4, D)
        for i in range(2):
            add_piece(f"g{l}_{i}", colblock(gate_w[l], i * 512, 512), 8, 512)
        add_piece(f"pw{l}", rowblock(ple_w[l], 0, 2), 2, D)
        if l == 1:
            for i in range(2):
                add_piece(f"kv_{i}", colblock(w_kv, i * 512, 512), 8, 512)
    NP = len(plist)
    wbf = nc.dram_tensor("wbf", [NP, 128, PIECE], BF16, kind="Internal").ap()

    def tile_order():
        o = []
        for l in range(DEPTH):
            if l < 2:
                o += [f"xbc{l}_{i}" for i in range(8)] + [f"z{l}_{i}" for i in range(4)] + [f"dt{l}"]
                o += [f"wo{l}_{i}" for i in range(4)]
            else:
                o += [f"q{l}", f"bo{l}"]
            o += [f"w1{l}_{i}" for i in range(8)] + [f"w2{l}_{i}" for i in range(8)]
            o += [f"g{l}_0", f"g{l}_1", f"pw{l}"]
            if l == 1:
                o += ["kv_0", "kv_1"]
        return o

    tiles = [("p", t * 512, 128, 4) for t in range(SEQ // 512)] + [("s", 0, TS, NSEQ_S)]
    worder = tile_order() * len(tiles)

    KLEN_S = PAST + 128
    kT_p = nc.dram_tensor("kT_p", [128, 4, SEQ], BF16, kind="Internal").ap()
    vv_p = nc.dram_tensor("vv_p", [SEQ, SBW], BF16, kind="Internal").ap()
    kT_s = nc.dram_tensor("kT_s", [NSEQ_S, 128, 4, KLEN_S], BF16, kind="Internal").ap()
    vv_s = nc.dram_tensor("vv_s", [NSEQ_S, KLEN_S, SBW], BF16, kind="Internal").ap()

    with ExitStack() as es:
        ASZ = 207 * 1024
        arena_t = es.enter_context(nc.sbuf_tensor("arena", [128, ASZ], mybir.dt.uint8))
        top = [0]

        def alloc(dt, *shape, p=128):
            n = int(np.prod(shape))
            sz = 4 if dt == F32 else 2
            off = top[0]
            top[0] += (n * sz + 63) // 64 * 64
            assert top[0] <= ASZ, (top[0], ASZ)
            v = arena_t[:, off:off + n * sz].bitcast(dt)
            if len(shape) == 2:
                v = v.rearrange("p (a b) -> p a b", a=shape[0])
            elif len(shape) == 3:
                v = v.rearrange("p (a b c) -> p a b c", a=shape[0], b=shape[1])
            return v

        PS = [es.enter_context(nc.psum_tensor(f"ps{i}", [128, 512], F32))[:, :] for i in range(8)]
        PK = [f"ps{i}" for i in range(8)]

        cbf = alloc(BF16, NCB); cf = alloc(F32, 384)
        ident_bf = cbf[:, 0:128]; GT_bf = cbf[:, 128:256]; LE_bf = cbf[:, 256:384]
        ones_bf = cbf[:, 384:512]
        M4 = cbf[:, 512:512 + 2048].rearrange("p (a b) -> p a b", a=4)
        ident_f = cf[:, 0:128]; LE_f = cf[:, 128:256]; ones_f = cf[:, 256:384]
        cone = alloc(F32, 1); ceps = alloc(F32, 1); crms = alloc(F32, 1)
        wslots = [alloc(BF16, PIECE) for _ in range(NSLOT)]
        x_tok = alloc(F32, 4, D)
        xT = alloc(BF16, 8, 512)
        lng = alloc(F32, D); lnb = alloc(F32, D)
        hT = [alloc(F32, DI) for _ in range(2)]
        hT_bf = alloc(BF16, DI)
        cst = [alloc(F32, 32, 3) for _ in range(2)]
        cw = alloc(F32, 32, 4); cb = alloc(F32, 32)
        dtb = alloc(F32, NH); Aneg = alloc(F32, NH); dsk = alloc(F32, NH)
        mv = alloc(F32, 2); stt = alloc(F32, 2, 6); rstd = alloc(F32, 1); ss = alloc(F32, 1)
        PERSIST_TOP = top[0]

        S.dma(cbf, cst_bf_d, [], ["cbf"], "cbf")
        S.dma(cf, cst_f_d, [], ["cf"], "cf")
        S.pool(lambda e: e.memset(cone, 1.0), [], ["cone"])
        S.pool(lambda e: e.memset(ceps, LN_EPS), [], ["ceps"])
        S.pool(lambda e: e.memset(crms, RMS_EPS), [], ["crms"])
        CON = ["cbf", "cf", "cone", "ceps", "crms"]

        for (name, src, a, b) in plist:
            pid = pieces[name][0]
            dst = wbf[pid][:, 0:a * b].rearrange("p (k n) -> p k n", k=a)
            S.dma(dst, src, [], [f"wbf{pid}"], "wconv", q="pool", persist=True)

        wstate = {"next_load": 0, "next_use": 0}

        def w_issue(upto):
            while wstate["next_load"] < min(upto, len(worder)):
                j = wstate["next_load"]
                name = worder[j]
                pid, a, b = pieces[name]
                sl = j % NSLOT
                rd = [f"wbf{p_}" for p_ in range(NP)] if j == 0 else []
                S.dma(wslots[sl][:, 0:a * b], wbf[pid][:, 0:a * b], rd, [f"wslot{sl}"], f"wslot{sl}", persist=True)
                wstate["next_load"] += 1

        def w_next(name, back=0):
            j = wstate["next_use"]
            assert worder[j] == name, (worder[j], name)
            w_issue(j - back + NSLOT)
            wstate["next_use"] += 1
            pid, a, b = pieces[name]
            sl = j % NSLOT
            return wslots[sl][:, 0:a * b].rearrange("p (k n) -> p k n", k=a), f"wslot{sl}"

        rot = {}

        def psr(lst):
            k = tuple(lst)
            rot[k] = rot.get(k, 0) + 1
            return lst[rot[k] % len(lst)]

        def transpose_to_xT(ct, nch, src_tok, src_key, dstT, dst_key, nkc, tmp_bf, tmp_key):
            for c in range(nch):
                S.pool(lambda e, c=c: e.tensor_copy(out=tmp_bf[:ct, 0:nkc * 128], in_=src_tok[:ct, c, 0:nkc * 128]), [src_key], [tmp_key])
                pi = psr([0, 7])
                pst = PS[pi].bitcast(BF16)
                for k in range(nkc):
                    S.pe(lambda e, k=k, pst=pst: e.transpose(out=pst[:, k * ct:(k + 1) * ct], in_=tmp_bf[:ct, k * 128:(k + 1) * 128], identity=ident_bf[:ct, :ct]),
                         [tmp_key, "cbf"], [PK[pi]])
                S.act(lambda e, c=c, pst=pst: e.activation(out=dstT[:, 0:nkc, c * ct:(c + 1) * ct], in_=pst[:, 0:nkc * ct].rearrange("p (k t) -> p k t", k=nkc), func=AF.Copy),
                      [PK[pi]], [dst_key])

        def layer_norm(ct, c, mix_banks, g_ap, b_ap, gkey, dst, dst_key, vbuf, vkey, src=None, src_key=None, alpha=None):
            if mix_banks is not None:
                for hb in range(2):
                    pi = mix_banks[hb]
                    S.dve(lambda e, hb=hb, pi=pi: e.scalar_tensor_tensor(out=vbuf[:ct, hb * 512:(hb + 1) * 512], in0=x_tok[:ct, c, hb * 512:(hb + 1) * 512], scalar=float(alpha),
                                                                          in1=PS[pi][:ct, :], op0=ALU.mult, op1=ALU.add), ["x_tok", PK[pi]], [vkey])
                v = vbuf[:ct, :]; vk = vkey
            else:
                v = src; vk = src_key
            for hb in range(2):
                S.dve(lambda e, hb=hb: e.bn_stats(out=stt[:ct, hb, :], in_=v[:, hb * 512:(hb + 1) * 512]), [vk], ["stt"])
            S.dve(lambda e: e.bn_aggr(out=mv[:ct, :], in_=stt[:ct, :, :]), ["stt"], ["mv"])
            S.act(lambda e: e.activation(out=rstd[:ct, :], in_=mv[:ct, 1:2], func=AF.Ln, bias=ceps[:ct, :], scale=1.0), ["mv", "ceps"], ["rstd"])
            S.act(lambda e: e.activation(out=rstd[:ct, :], in_=rstd[:ct, :], func=AF.Exp, scale=-0.5), ["rstd"], ["rstd"])
            S.dve(lambda e: e.tensor_scalar(out=vbuf[:ct, :], in0=v, scalar1=mv[:ct, 0:1], scalar2=rstd[:ct, :], op0=ALU.subtract, op1=ALU.mult),
                  [vk, "mv", "rstd"], [vkey])
            S.pool(lambda e: e.tensor_tensor(out=vbuf[:ct, :], in0=vbuf[:ct, :], in1=g_ap[:ct, :], op=ALU.mult), [vkey, gkey], [vkey])
            S.pool(lambda e: e.tensor_tensor(out=dst, in0=vbuf[:ct, :], in1=b_ap[:ct, :], op=ALU.add), [vkey, gkey], [dst_key])

        def load_ln(g_d, b_d):
            S.dma(lng, g_d.partition_broadcast(128), [], ["lnp"], "lnp")
            S.dma(lnb, b_d.partition_broadcast(128), [], ["lnp"], "lnp")

        top[0] = PERSIST_TOP
        ktok = alloc(BF16, SBW); kst = alloc(BF16, 4, 128)
        zpad = alloc(BF16, SBW)
        vtok_ = alloc(BF16, SBW)
        S.pool(lambda e: e.memset(zpad, 0.0), [], ["zpad"])
        for s in range(NSEQ_S):
            S.dma(kT_s[s][:, :, PAST:PAST + 128], zpad.rearrange("p (h t) -> p h t", h=4), ["zpad"], [f"kT_s{s}"], f"zk{s}", q="pool")
            S.dma(vv_s[s][PAST:PAST + 128, :], zpad, ["zpad"], [f"vv_s{s}"], f"zv{s}", q="pool")
            for kb in range(PAST // 128):
                S.dma(vtok_, c_v[s][kb * 128:(kb + 1) * 128, :], [], ["vtok_"], "vtok_", q="pool")
                S.dma(vv_s[s][kb * 128:(kb + 1) * 128, :], vtok_, ["vtok_"], [f"vv_s{s}"], "vtok_o", q="pool")
                S.dma(ktok, c_k[s][kb * 128:(kb + 1) * 128, :], [], ["ktok"], "ktok", q="pool")
                pst = PS[0].bitcast(BF16)
                for h in range(4):
                    S.pe(lambda e, h=h, pst=pst: e.transpose(out=pst[:, h * 128:(h + 1) * 128], in_=ktok[:, h * 128:(h + 1) * 128], identity=ident_bf), ["ktok", "cbf"], [PK[0]])
                S.act(lambda e, pst=pst: e.activation(out=kst, in_=pst[:, 0:512].rearrange("p (h t) -> p h t", h=4), func=AF.Copy), [PK[0]], ["kst"])
                S.dma(kT_s[s][:, :, kb * 128:(kb + 1) * 128], kst, ["kst"], [f"kT_s{s}"], "kst", q="pool")
        S.barrier()

        def mamba_mixer(l, kind, t0, ct, nch, tix):
            TT = ct * nch
            nsq, L = (1, TT) if kind == "p" else (nch, ct)
            first = (kind == "p" and t0 == 0)
            last = (kind == "p" and t0 + TT == SEQ)
            top[0] = PERSIST_TOP
            convin = [alloc(F32, nsq, 3 + L) for _ in range(2)]
            acc = [alloc(F32, nsq, L) for _ in range(2)]
            xcT = alloc(BF16, 16, 512)
            BT = alloc(BF16, 8, 512); CT = alloc(BF16, 8, 512)
            xtc = alloc(BF16, 4, DI); Btok = alloc(BF16, 4, NG * NS)
            sz = alloc(BF16, 4, DI)
            normg = alloc(F32, 16)
            dtt = alloc(F32, 4, NH); dtA = alloc(F32, 4, NH); dthi = alloc(BF16, 4, NH); dtlo = alloc(BF16, 4, NH); tmp32 = alloc(F32, NH)
            cum = alloc(F32, NH); ecum = alloc(F32, NH); etot = alloc(F32, NH); wend = alloc(F32, NH)
            xw = alloc(BF16, DI)
            Rhi = [alloc(BF16, 4, 128) for _ in range(2)]; Rlo = [alloc(BF16, 4, 128) for _ in range(2)]
            E_ = [alloc(F32, 4, 128) for _ in range(2)]; cbm_ = [alloc(F32, 128) for _ in range(2)]; mT_ = [alloc(BF16, 4, 128) for _ in range(2)]
            yo_ = [alloc(F32, 256) for _ in range(2)]; y1_ = [alloc(F32, 256) for _ in range(2)]; yn_ = [alloc(BF16, 256) for _ in range(2)]; junk_ = [alloc(F32, 256) for _ in range(2)]
            ss_ = [alloc(F32, 1) for _ in range(2)]
            stg = alloc(F32, 16, 128)
            cstl = alloc(F32, nsq, 32, 3)
            yT = xcT

            for k_ in range(4):
                S.dma(cw[:, :, k_], a_conv_w[l, k_].rearrange("(c p) -> p c", p=128), [], ["cw"], "cw", allow_slow_non_contiguous=True)
            S.dma(cb, a_conv_b[l].rearrange("(c p) -> p c", p=128), [], ["cb"], "cb", allow_slow_non_contiguous=True)
            S.dma(dtb, a_dt_bias[l].partition_broadcast(128), [], ["dtb"], "dtb")
            S.dma(Aneg, a_A_log[l].partition_broadcast(128), [], ["Aneg"], "Aneg")
            S.dma(dsk, a_d_skip[l].partition_broadcast(128), [], ["dsk"], "dsk")
            S.dma(normg, a_norm_g[l].rearrange("(c p) -> p c", p=128), [], ["normg"], "normg", allow_slow_non_contiguous=True)
            S.act(lambda e: e.activation(out=Aneg, in_=Aneg, func=AF.Exp), ["Aneg"], ["Aneg"])
            S.dve(lambda e: e.tensor_scalar(out=Aneg, in0=Aneg, scalar1=-1.0, scalar2=None, op0=ALU.mult), ["Aneg"], ["Aneg"])
            load_ln(ln1_g[l], ln1_b[l])
            if kind == "s":
                for s_ in range(nsq):
                    for k_ in range(3):
                        for q4 in range(2):
                            S.dma(cstl[:, s_, q4 * 16:(q4 + 1) * 16, k_], st_conv[l, s_, k_][q4 * 2048:(q4 + 1) * 2048].rearrange("(c p) -> p c", p=128),
                                  [], ["cstl"], "cstl", allow_slow_non_contiguous=True)
            elif first:
                S.pool(lambda e: e.memset(cst[l], 0.0), [], [f"cst{l}"])

            for i in range(8):
                wv, wk = w_next(f"xbc{l}_{i}")
                for j in range(4):
                    cc = i * 4 + j
                    pi = psr([1, 2])
                    for kc in range(8):
                        S.pe(lambda e, kc=kc, j=j, wv=wv, pi=pi: e.matmul(PS[pi][:, 0:TT], lhsT=wv[:, kc, j * 128:(j + 1) * 128], rhs=xT[:, kc, 0:TT], start=(kc == 0), stop=(kc == 7)),
                             [wk, "xT"], [PK[pi]])
                    b = cc % 2
                    ci = convin[b]; ck = f"convin{b}"; ac = acc[b]; ak = f"acc{b}"
                    S.act(lambda e, ci=ci, pi=pi: e.activation(out=ci[:, :, 3:3 + L], in_=PS[pi][:, 0:TT].rearrange("p (s t) -> p s t", s=nsq), func=AF.Copy), [PK[pi]], [ck])
                    if kind == "p":
                        S.pool(lambda e, ci=ci, cc=cc: e.tensor_copy(out=ci[:, 0, 0:3], in_=cst[l][:, cc, :]), [f"cst{l}"], [ck])
                    else:
                        S.pool(lambda e, ci=ci, cc=cc: e.tensor_copy(out=ci[:, :, 0:3], in_=cstl[:, :, cc, :]), ["cstl"], [ck])
                    S.dve(lambda e, ci=ci, ac=ac, cc=cc: e.tensor_scalar(out=ac, in0=ci[:, :, 0:L], scalar1=cw[:, cc, 0:1], scalar2=None, op0=ALU.mult), [ck, "cw"], [ak])
                    for k in range(1, 4):
                        S.dve(lambda e, ci=ci, ac=ac, cc=cc, k=k: e.scalar_tensor_tensor(out=ac, in0=ci[:, :, k:k + L], scalar=cw[:, cc, k:k + 1], in1=ac, op0=ALU.mult, op1=ALU.add),
                              [ck, "cw", ak], [ak])
                    if cc < 16:
                        dst = xcT[:, cc, 0:TT]; dk = "xcT"
                    elif cc < 24:
                        dst = BT[:, cc - 16, 0:TT]; dk = "BT"
                    else:
                        dst = CT[:, cc - 24, 0:TT]; dk = "CT"
                    S.act(lambda e, ac=ac, dst=dst, cc=cc: e.activation(out=dst.rearrange("p (s t) -> p s t", s=nsq), in_=ac, func=AF.Silu, bias=cb[:, cc:cc + 1], scale=1.0), [ak, "cb"], [dk])
                    if kind == "p":
                        S.pool(lambda e, ci=ci, cc=cc: e.tensor_copy(out=cst[l][:, cc, :], in_=ci[:, 0, L:L + 3]), [ck], [f"cst{l}"])
                    else:
                        S.pool(lambda e, ci=ci, cc=cc: e.tensor_copy(out=cstl[:, :, cc, :], in_=ci[:, :, L:L + 3]), [ck], ["cstl"])
            if kind == "s":
                for s_ in range(nsq):
                    for k_ in range(3):
                        for q4 in range(2):
                            S.dma(conv_s[l, s_, k_][q4 * 2048:(q4 + 1) * 2048].rearrange("(c p) -> p c", p=128), cstl[:, s_, q4 * 16:(q4 + 1) * 16, k_],
                                  ["cstl"], ["conv_out"], "cstl", q="pool", allow_slow_non_contiguous=True)
            elif last:
                for k_ in range(3):
                    for q4 in range(2):
                        S.dma(conv_p[l, k_][q4 * 2048:(q4 + 1) * 2048].rearrange("(c p) -> p c", p=128), cst[l][:, q4 * 16:(q4 + 1) * 16, k_],
                              [f"cst{l}"], ["conv_out"], f"cst{l}", q="pool", allow_slow_non_contiguous=True)

            for c in range(nch):
                for half in range(2):
                    pi = psr([3, 4])
                    pst = PS[pi].bitcast(BF16)
                    for j in range(8):
                        S.pe(lambda e, j=j, pst=pst, c=c, half=half: e.transpose(out=pst[:ct, j * 128:(j + 1) * 128], in_=xcT[:, half * 8 + j, c * ct:(c + 1) * ct], identity=ident_bf),
                             ["xcT", "cbf"], [PK[pi]])
                    S.act(lambda e, pst=pst, c=c, half=half: e.activation(out=xtc[:ct, c, half * 1024:(half + 1) * 1024], in_=pst[:ct, 0:1024], func=AF.Copy), [PK[pi]], ["xtc"])
                pi = psr([3, 4])
                pst = PS[pi].bitcast(BF16)
                for j in range(8):
                    S.pe(lambda e, j=j, pst=pst, c=c: e.transpose(out=pst[:ct, j * 128:(j + 1) * 128], in_=BT[:, j, c * ct:(c + 1) * ct], identity=ident_bf), ["BT", "cbf"], [PK[pi]])
                S.act(lambda e, pst=pst, c=c: e.activation(out=Btok[:ct, c, :], in_=pst[:ct, 0:1024], func=AF.Copy), [PK[pi]], ["Btok"])

            for i in range(4):
                wv, wk = w_next(f"z{l}_{i}")
                for c in range(nch):
                    pi = psr([5, 6])
                    for kc in range(8):
                        S.pe(lambda e, kc=kc, c=c, wv=wv, pi=pi: e.matmul(PS[pi][:ct, :], lhsT=xT[:, kc, c * ct:(c + 1) * ct], rhs=wv[:, kc, :], start=(kc == 0), stop=(kc == 7)),
                             [wk, "xT"], [PK[pi]])
                    S.act(lambda e, c=c, i=i, pi=pi: e.activation(out=sz[:ct, c, i * 512:(i + 1) * 512], in_=PS[pi][:ct, :], func=AF.Silu), [PK[pi]], ["sz"])
            wv, wk = w_next(f"dt{l}")
            for c in range(nch):
                pi = psr([5, 6])
                for kc in range(8):
                    S.pe(lambda e, kc=kc, c=c, wv=wv, pi=pi: e.matmul(PS[pi][:ct, 0:NH], lhsT=xT[:, kc, c * ct:(c + 1) * ct], rhs=wv[:, kc, :], start=(kc == 0), stop=(kc == 7)),
                         [wk, "xT"], [PK[pi]])
                S.dve(lambda e, c=c, pi=pi: e.tensor_tensor(out=dtt[:ct, c, :], in0=PS[pi][:ct, 0:NH], in1=dtb[:ct, :], op=ALU.add), [PK[pi], "dtb"], ["dtt"])
            S.act(lambda e: e.activation(out=dtt[:ct, 0:nch, :], in_=dtt[:ct, 0:nch, :], func=AF.Exp), ["dtt"], ["dtt"])
            S.act(lambda e: e.activation(out=dtt[:ct, 0:nch, :], in_=dtt[:ct, 0:nch, :], func=AF.Ln, bias=cone[:ct, :], scale=1.0), ["dtt", "cone"], ["dtt"])
            S.dve(lambda e: e.tensor_tensor(out=dtA[:ct, 0:nch, :], in0=dtt[:ct, 0:nch, :], in1=Aneg[:ct, :].unsqueeze(1).broadcast_to([ct, nch, NH]), op=ALU.mult), ["dtt", "Aneg"], ["dtA"])
            S.dve(lambda e: e.tensor_copy(out=dthi[:ct, 0:nch, :], in_=dtA[:ct, 0:nch, :]), ["dtA"], ["dthi"])
            S.dve(lambda e: e.tensor_tensor(out=dtlo[:ct, 0:nch, :], in0=dtA[:ct, 0:nch, :], in1=dthi[:ct, 0:nch, :], op=ALU.subtract), ["dtA", "dthi"], ["dtlo"])

            hk = f"hT{l}"
            for c in range(nch):
                if kind == "s":
                    S.dma(stg, st_ssm[l, c].rearrange("(j q) n -> q j n", q=128), [], ["stg"], "stg")
                    for j in range(16):
                        pi = psr([3, 4])
                        S.pe(lambda e, j=j, pi=pi: e.transpose(out=PS[pi][:, 0:128], in_=stg[:, j, :], identity=ident_f), ["stg", "cf"], [PK[pi]])
                        S.act(lambda e, j=j, pi=pi: e.activation(out=hT[l][:, j * 128:(j + 1) * 128], in_=PS[pi][:, 0:128], func=AF.Copy), [PK[pi]], [hk])
                elif first and c == 0:
                    S.pool(lambda e: e.memset(hT[l], 0.0), [], [hk])
                S.pool(lambda e: e.tensor_copy(out=hT_bf, in_=hT[l]), [hk], ["hT_bf"])
                S.pe(lambda e, c=c: e.matmul(PS[7][:ct, 0:NH], lhsT=LE_f[:ct, :ct], rhs=dtA[:ct, c, :], start=True, stop=True), ["cf", "dtA"], [PK[7]])
                S.pe(lambda e, c=c: e.matmul(PS[7][:, NH:2 * NH], lhsT=ones_f[:ct, :], rhs=dtA[:ct, c, :], start=True, stop=True), ["cf", "dtA"], [PK[7]])
                S.act(lambda e: e.activation(out=cum[:ct, :], in_=PS[7][:ct, 0:NH], func=AF.Copy), [PK[7]], ["cum"])
                S.act(lambda e: e.activation(out=ecum[:ct, :], in_=PS[7][:ct, 0:NH], func=AF.Exp), [PK[7]], ["ecum"])
                S.act(lambda e: e.activation(out=etot, in_=PS[7][:, NH:2 * NH], func=AF.Exp), [PK[7]], ["etot"])
                S.dve(lambda e: e.tensor_tensor(out=wend[:ct, :], in0=PS[7][:ct, NH:2 * NH], in1=cum[:ct, :], op=ALU.subtract), [PK[7], "cum"], ["wend"])
                S.act(lambda e: e.activation(out=wend[:ct, :], in_=wend[:ct, :], func=AF.Exp), ["wend"], ["wend"])
                S.dve(lambda e, c=c: e.tensor_tensor(out=wend[:ct, :], in0=wend[:ct, :], in1=dtt[:ct, c, :], op=ALU.mult), ["wend", "dtt"], ["wend"])
                S.dve(lambda e, c=c: e.tensor_tensor(out=xw[:ct, :].rearrange("p (h q) -> p h q", h=NH), in0=xtc[:ct, c, :].rearrange("p (h q) -> p h q", h=NH),
                                                     in1=wend[:ct, :].unsqueeze(2).broadcast_to([ct, NH, HP]), op=ALU.mult), ["xtc", "wend"], ["xw"])
                for g in range(NG):
                    pP = psr([1, 2]); pQ = 3 + (g % 2); pR = 5 + (g % 2)
                    gb = g % 2
                    E = E_[gb]; cbm = cbm_[gb]; mT = mT_[gb]; yo = yo_[gb]; y1 = y1_[gb]; yn = yn_[gb]; junk = junk_[gb]; ssg = ss_[gb]
                    kE = f"E{gb}"; kcbm = f"cbm{gb}"; kmT = f"mT{gb}"; kyo = f"yo{gb}"; ky1 = f"y1{gb}"; kyn = f"yn{gb}"; kjunk = f"junk{gb}"; kss = f"ssg{gb}"
                    pT_ = psr([0, 7])
                    for (Rb, Rk, dsrc, dk_) in ((Rhi[gb], f"Rhi{gb}", dthi, "dthi"), (Rlo[gb], f"Rlo{gb}", dtlo, "dtlo")):
                        S.pool(lambda e, E=E, cbm=cbm, mT=mT, yo=yo, y1=y1, yn=yn, junk=junk, ssg=ssg, Rb=Rb, dsrc=dsrc, c=c, g=g: e.tensor_tensor(out=Rb[:ct, :, 0:ct], in0=dsrc[:ct, c, g * 4:(g + 1) * 4].unsqueeze(2).broadcast_to([ct, 4, ct]),
                                                                              in1=LE_bf[:ct, 0:ct].unsqueeze(1).broadcast_to([ct, 4, ct]), op=ALU.mult), [dk_, "cbf"], [Rk])
                    for ri, (Rb, Rk) in enumerate(((Rhi[gb], f"Rhi{gb}"), (Rlo[gb], f"Rlo{gb}"))):
                        S.pe(lambda e, E=E, cbm=cbm, mT=mT, yo=yo, y1=y1, yn=yn, junk=junk, ssg=ssg, Rb=Rb, ri=ri, pP=pP: e.matmul(PS[pP][:ct, 0:4 * ct].rearrange("p (h t) -> p h t", h=4), lhsT=GT_bf[:ct, :ct], rhs=Rb[:ct, :, 0:ct],
                                                                      start=(ri == 0), stop=(ri == 1)), [Rk, "cbf"], [PK[pP]])
                    S.pe(lambda e, E=E, cbm=cbm, mT=mT, yo=yo, y1=y1, yn=yn, junk=junk, ssg=ssg, g=g, c=c, pQ=pQ: e.matmul(PS[pQ][:ct, 0:ct], lhsT=BT[:, g, c * ct:(c + 1) * ct], rhs=CT[:, g, c * ct:(c + 1) * ct], start=True, stop=True), ["BT", "CT"], [PK[pQ]])
                    S.act(lambda e, E=E, cbm=cbm, mT=mT, yo=yo, y1=y1, yn=yn, junk=junk, ssg=ssg, pP=pP: e.activation(out=E[:ct, :, 0:ct], in_=PS[pP][:ct, 0:4 * ct].rearrange("p (h t) -> p h t", h=4), func=AF.Exp), [PK[pP]], [kE])
                    S.dve(lambda e, E=E, cbm=cbm, mT=mT, yo=yo, y1=y1, yn=yn, junk=junk, ssg=ssg, pQ=pQ: e.tensor_tensor(out=cbm[:ct, 0:ct], in0=PS[pQ][:ct, 0:ct], in1=LE_bf[:ct, 0:ct], op=ALU.mult), [PK[pQ], "cbf"], [kcbm])
                    for h in range(4):
                        hh = g * 4 + h
                        S.dve(lambda e, E=E, cbm=cbm, mT=mT, yo=yo, y1=y1, yn=yn, junk=junk, ssg=ssg, h=h, hh=hh, c=c: e.scalar_tensor_tensor(out=mT[:ct, h, 0:ct], in0=E[:ct, h, 0:ct], scalar=dtt[:ct, c, hh:hh + 1], in1=cbm[:ct, 0:ct], op0=ALU.mult, op1=ALU.mult),
                              [kE, "dtt", kcbm], [kmT])
                    for h in range(4):
                        hh = g * 4 + h
                        S.pe(lambda e, E=E, cbm=cbm, mT=mT, yo=yo, y1=y1, yn=yn, junk=junk, ssg=ssg, h=h, hh=hh, c=c, pQ=pQ: e.matmul(PS[pQ][:ct, 128 + h * HP:128 + (h + 1) * HP], lhsT=mT[:ct, h, 0:ct], rhs=xtc[:ct, c, hh * HP:(hh + 1) * HP], start=True, stop=True),
                             [kmT, "xtc"], [PK[pQ]])
                    S.pe(lambda e, E=E, cbm=cbm, mT=mT, yo=yo, y1=y1, yn=yn, junk=junk, ssg=ssg, g=g, c=c, pR=pR: e.matmul(PS[pR][:ct, 0:256], lhsT=CT[:, g, c * ct:(c + 1) * ct], rhs=hT_bf[:, g * 256:(g + 1) * 256], start=True, stop=True), ["CT", "hT_bf"], [PK[pR]])
                    S.dve(lambda e, E=E, cbm=cbm, mT=mT, yo=yo, y1=y1, yn=yn, junk=junk, ssg=ssg, g=g, pR=pR: e.tensor_tensor(out=yo[:ct, :].rearrange("p (h q) -> p h q", h=4), in0=PS[pR][:ct, 0:256].rearrange("p (h q) -> p h q", h=4),
                                                              in1=ecum[:ct, g * 4:(g + 1) * 4].unsqueeze(2).broadcast_to([ct, 4, HP]), op=ALU.mult), [PK[pR], "ecum"], [kyo])
                    S.dve(lambda e, E=E, cbm=cbm, mT=mT, yo=yo, y1=y1, yn=yn, junk=junk, ssg=ssg, pQ=pQ: e.tensor_tensor(out=y1[:ct, :], in0=PS[pQ][:ct, 128:384], in1=yo[:ct, :], op=ALU.add), [PK[pQ], kyo], [ky1])
                    S.pool(lambda e, E=E, cbm=cbm, mT=mT, yo=yo, y1=y1, yn=yn, junk=junk, ssg=ssg, g=g, c=c: e.tensor_tensor(out=yo[:ct, :].rearrange("p (h q) -> p h q", h=4), in0=xtc[:ct, c, g * 256:(g + 1) * 256].rearrange("p (h q) -> p h q", h=4),
                                                              in1=dsk[:ct, g * 4:(g + 1) * 4].unsqueeze(2).broadcast_to([ct, 4, HP]), op=ALU.mult), ["xtc", "dsk", ky1], [kyo])
                    S.dve(lambda e, E=E, cbm=cbm, mT=mT, yo=yo, y1=y1, yn=yn, junk=junk, ssg=ssg: e.tensor_tensor(out=y1[:ct, :], in0=y1[:ct, :], in1=yo[:ct, :], op=ALU.add), [ky1, kyo], [ky1])
                    S.dve(lambda e, E=E, cbm=cbm, mT=mT, yo=yo, y1=y1, yn=yn, junk=junk, ssg=ssg, g=g, c=c: e.tensor_tensor(out=y1[:ct, :], in0=y1[:ct, :], in1=sz[:ct, c, g * 256:(g + 1) * 256], op=ALU.mult), [ky1, "sz"], [ky1])
                    S.pool(lambda e, E=E, cbm=cbm, mT=mT, yo=yo, y1=y1, yn=yn, junk=junk, ssg=ssg: e.tensor_tensor(out=junk[:ct, :], in0=y1[:ct, :], in1=y1[:ct, :], op=ALU.mult), [ky1], [kjunk])
                    S.dve(lambda e, E=E, cbm=cbm, mT=mT, yo=yo, y1=y1, yn=yn, junk=junk, ssg=ssg: e.tensor_reduce(out=ssg[:ct, :], in_=junk[:ct, :], axis=mybir.AxisListType.X, op=ALU.add), [kjunk], [kss])
                    S.act(lambda e, E=E, cbm=cbm, mT=mT, yo=yo, y1=y1, yn=yn, junk=junk, ssg=ssg: e.activation(out=ssg[:ct, :], in_=ssg[:ct, :], func=AF.Ln, bias=crms[:ct, :], scale=1.0 / 256.0), [kss, "crms"], [kss])
                    S.act(lambda e, E=E, cbm=cbm, mT=mT, yo=yo, y1=y1, yn=yn, junk=junk, ssg=ssg: e.activation(out=ssg[:ct, :], in_=ssg[:ct, :], func=AF.Exp, scale=-0.5), [kss], [kss])
                    S.dve(lambda e, E=E, cbm=cbm, mT=mT, yo=yo, y1=y1, yn=yn, junk=junk, ssg=ssg: e.tensor_scalar(out=yn[:ct, :], in0=y1[:ct, :], scalar1=ssg[:ct, :], scalar2=None, op0=ALU.mult), [ky1, kss], [kyn])
                    pst = PS[pT_].bitcast(BF16)
                    for j in range(2):
                        S.pe(lambda e, E=E, cbm=cbm, mT=mT, yo=yo, y1=y1, yn=yn, junk=junk, ssg=ssg, j=j, pst=pst: e.transpose(out=pst[:, j * ct:(j + 1) * ct], in_=yn[:ct, j * 128:(j + 1) * 128], identity=ident_bf[:ct, :ct]), [kyn, "cbf"], [PK[pT_]])
                    for j in range(2):
                        S.act(lambda e, E=E, cbm=cbm, mT=mT, yo=yo, y1=y1, yn=yn, junk=junk, ssg=ssg, g=g, c=c, j=j, pst=pst: e.activation(out=yT[:, 2 * g + j, c * ct:(c + 1) * ct], in_=pst[:, j * ct:(j + 1) * ct], func=AF.Copy, scale=normg[:, 2 * g + j:2 * g + j + 1]), [PK[pT_], "normg"], ["xcT"])
                    S.pe(lambda e, E=E, cbm=cbm, mT=mT, yo=yo, y1=y1, yn=yn, junk=junk, ssg=ssg, g=g, c=c, pR=pR: e.matmul(PS[pR][:, 256:512], lhsT=Btok[:ct, c, g * 128:(g + 1) * 128], rhs=xw[:ct, g * 256:(g + 1) * 256], start=True, stop=True), ["Btok", "xw"], [PK[pR]])
                    S.pool(lambda e, E=E, cbm=cbm, mT=mT, yo=yo, y1=y1, yn=yn, junk=junk, ssg=ssg, g=g: e.tensor_tensor(out=hT[l][:, g * 256:(g + 1) * 256].rearrange("p (h q) -> p h q", h=4), in0=hT[l][:, g * 256:(g + 1) * 256].rearrange("p (h q) -> p h q", h=4),
                                                         in1=etot[:, g * 4:(g + 1) * 4].unsqueeze(2).broadcast_to([128, 4, HP]), op=ALU.mult), [hk, "etot", "hT_bf"], [hk])
                    S.dve(lambda e, E=E, cbm=cbm, mT=mT, yo=yo, y1=y1, yn=yn, junk=junk, ssg=ssg, g=g, pR=pR: e.tensor_tensor(out=hT[l][:, g * 256:(g + 1) * 256], in0=PS[pR][:, 256:512], in1=hT[l][:, g * 256:(g + 1) * 256], op=ALU.add), [PK[pR], hk], [hk])
                dsts = None
                if kind == "s":
                    dsts = ssm_s[l, c]
                elif last and c == nch - 1:
                    dsts = ssm_p[l]
                if dsts is not None:
                    for j in range(16):
                        pi = psr([3, 4])
                        S.pe(lambda e, j=j, pi=pi: e.transpose(out=PS[pi][:, 0:128], in_=hT[l][:, j * 128:(j + 1) * 128], identity=ident_f), [hk, "cf"], [PK[pi]])
                        S.act(lambda e, j=j, pi=pi: e.activation(out=stg[:, j, :], in_=PS[pi][:, 0:128], func=AF.Copy), [PK[pi]], ["stg"])
                    S.dma(dsts.rearrange("(j q) n -> q j n", q=128), stg, ["stg"], ["ssm_out"], "stg", q="pool")

            S.barrier()
            vb = stg[:, 0:8, :].rearrange("p a b -> p (a b)")
            for i in range(4):
                wv, wk = w_next(f"wo{l}_{i}")
                for c in range(nch):
                    for hb in range(2):
                        pi = c * 2 + hb
                        for k in range(4):
                            S.pe(lambda e, c=c, hb=hb, k=k, i=i, wv=wv, pi=pi: e.matmul(PS[pi][:ct, :], lhsT=yT[:, i * 4 + k, c * ct:(c + 1) * ct], rhs=wv[:, k, hb * 512:(hb + 1) * 512],
                                                                                 start=(i == 0 and k == 0), stop=(i == 3 and k == 3)), [wk, "xcT"], [PK[pi]])
            for c in range(nch):
                layer_norm(ct, c, (c * 2, c * 2 + 1), lng, lnb, "lnp", x_tok[:ct, c, :], "x_tok", vb, "vb", alpha=ALPHA)
            S.barrier()

        def attn_mixer(l, kind, t0, ct, nch, tix):
            TT = ct * nch
            top[0] = PERSIST_TOP
            QT = alloc(BF16, 4, 512)
            oT = alloc(BF16, 4, 512)
            KS = [alloc(BF16, 4, 512) for _ in range(2)]
            VS = [alloc(BF16, 4, SBW) for _ in range(2)]
            eb = [alloc(F32, 512) for _ in range(2)]
            spb = [alloc(F32, 512) for _ in range(2)]
            spbf = [alloc(BF16, 512) for _ in range(2)]
            tb = [alloc(F32, 512) for _ in range(2)]
            wTb = [alloc(BF16, 512) for _ in range(2)]
            Aac = [alloc(F32, 512) for _ in range(2)]
            Ahi = [alloc(BF16, 512) for _ in range(2)]
            Alo = [alloc(BF16, 512) for _ in range(2)]
            vb = alloc(F32, D)
            DBG.update(dict(QT=QT, oT=oT, KS0=KS[0], KS1=KS[1], VS0=VS[0], VS1=VS[1], eb0=eb[0], sp0=spb[0], tb0=tb[0], wT0=wTb[0], eb1=eb[1], sp1=spb[1], tb1=tb[1], wT1=wTb[1], A0=Aac[0], A1=Aac[1], xT=xT))
            load_ln(ln1_g[l], ln1_b[l])
            scale = float(128 ** -0.5)
            wv, wk = w_next(f"q{l}")
            for h in range(4):
                pi = psr([6, 7])
                for kc in range(8):
                    S.pe(lambda e, kc=kc, h=h, wv=wv, pi=pi: e.matmul(PS[pi][:, 0:TT], lhsT=wv[:, kc, h * 128:(h + 1) * 128], rhs=xT[:, kc, 0:TT], start=(kc == 0), stop=(kc == 7)), [wk, "xT"], [PK[pi]])
                S.act(lambda e, h=h, pi=pi: e.activation(out=QT[:, h, 0:TT], in_=PS[pi][:, 0:TT], func=AF.Copy, scale=scale), [PK[pi]], ["QT"])
            if kind == "p":
                nkb = (t0 + TT) // 128
                units = [(kT_p, vv_p, "kT_p", "vv_p", nkb, 0, TT, None)]
            else:
                units = [(kT_s[s_], vv_s[s_], f"kT_s{s_}", f"vv_s{s_}", PAST // 128 + 1, s_ * ct, ct, ct) for s_ in range(nch)]
            if os.environ.get("REV_UNITS"):
                units = units[::-1]
            def do_unit(kTd, vd, kTk, vk, nkb, q0, nq, lastn):
                ngr = (nkb + 3) // 4
                for hp in range(2):
                    first = True
                    for hi in range(2):
                        S.pool(lambda e, hi=hi: e.memset(Aac[hi], 0.0), [], [f"A{hi}"])
                    for gi in range(ngr - 1, -1, -1):
                        sl = psr([0, 1])
                        kb0 = gi * 4
                        nb_here = min(4, nkb - kb0)
                        nkeys = nb_here * 128
                        S.dma(KS[sl][:, :, 0:nkeys], kTd[:, :, kb0 * 128:kb0 * 128 + nkeys], [kTk], [f"KS{sl}"], f"KS{sl}")
                        if nkeys % 128 == 0:
                            S.dma(VS[sl][:, 0:nb_here, :], vd[kb0 * 128:kb0 * 128 + nkeys, :].rearrange("(b p) c -> p b c", p=128), [vk], [f"VS{sl}"], f"VS{sl}")
                        else:
                            assert nb_here == 1
                            S.dma(VS[sl][:nkeys, 0, :], vd[kb0 * 128:kb0 * 128 + nkeys, :], [vk], [f"VS{sl}"], f"VS{sl}")
                        for bi in range(nb_here - 1, -1, -1):
                            kb = kb0 + bi
                            nk = 128
                            mask = None
                            if kind == "p":
                                kl = kb - t0 // 128
                                if kl >= 0:
                                    mask = M4[:, kl, 0:nq]
                            else:
                                if kb == nkb - 1:
                                    mask = M4[:, 0, 0:nq]
                            for hi in range(2):
                                h = hp * 2 + hi
                                pz = psr([6, 7]); pB = 2 + hi; pO = 4 + hi
                                b2 = psr([0, 1])
                                e_ = eb[b2]; sp_ = spb[b2]; spf = spbf[b2]; t_ = tb[b2]; w_ = wTb[b2]
                                S.pe(lambda e, h=h, bi=bi, nk=nk, pz=pz, sl=sl: e.matmul(PS[pz][:nk, 0:nq], lhsT=KS[sl][:, h, bi * 128:bi * 128 + nk], rhs=QT[:, h, q0:q0 + nq], start=True, stop=True),
                                     [f"KS{sl}", "QT"], [PK[pz]])
                                S.act(lambda e, e_=e_, nk=nk, pz=pz: e.activation(out=e_[:nk, 0:nq], in_=PS[pz][:nk, 0:nq], func=AF.Exp), [PK[pz]], [f"eb{b2}"])
                                S.act(lambda e, e_=e_, sp_=sp_, nk=nk: e.activation(out=sp_[:nk, 0:nq], in_=e_[:nk, 0:nq], func=AF.Ln, bias=cone[:nk, :], scale=1.0), [f"eb{b2}", "cone"], [f"sp{b2}"])
                                if mask is not None:
                                    S.dve(lambda e, sp_=sp_, nk=nk, mask=mask: e.tensor_tensor(out=sp_[:nk, 0:nq], in0=sp_[:nk, 0:nq], in1=mask, op=ALU.mult), [f"sp{b2}", "cbf"], [f"sp{b2}"])
                                S.pool(lambda e, sp_=sp_, spf=spf, nk=nk: e.tensor_copy(out=spf[:nk, 0:nq], in_=sp_[:nk, 0:nq]), [f"sp{b2}"], [f"spf{b2}"])
                                S.pe(lambda e, spf=spf, nk=nk, pB=pB, first=first: e.matmul(PS[pB][:, 0:nq], lhsT=GT_bf[:nk, :], rhs=spf[:nk, 0:nq], start=True, stop=first), [f"spf{b2}", "cbf"], [PK[pB]])
                                if not first:
                                    S.pe(lambda e, pB=pB, hi=hi: e.matmul(PS[pB][:, 0:nq], lhsT=ones_bf, rhs=Ahi[hi][:, 0:nq], start=False, stop=False), [f"Ahi{hi}", "cbf"], [PK[pB]])
                                    S.pe(lambda e, pB=pB, hi=hi: e.matmul(PS[pB][:, 0:nq], lhsT=ones_bf, rhs=Alo[hi][:, 0:nq], start=False, stop=True), [f"Alo{hi}", "cbf"], [PK[pB]])
                                S.dve(lambda e, t_=t_, sp_=sp_, nk=nk, pz=pz: e.tensor_tensor(out=t_[:nk, 0:nq], in0=PS[pz][:nk, 0:nq], in1=sp_[:nk, 0:nq], op=ALU.subtract), [PK[pz], f"sp{b2}"], [f"tb{b2}"])
                                S.dve(lambda e, t_=t_, nk=nk, pB=pB: e.tensor_tensor(out=t_[:nk, 0:nq], in0=t_[:nk, 0:nq], in1=PS[pB][:nk, 0:nq], op=ALU.subtract), [f"tb{b2}", PK[pB]], [f"tb{b2}"])
                                S.pool(lambda e, sp_=sp_, nk=nk, hi=hi: e.tensor_tensor(out=Aac[hi][:nk, 0:nq], in0=Aac[hi][:nk, 0:nq], in1=sp_[:nk, 0:nq], op=ALU.add), [f"A{hi}", f"sp{b2}"], [f"A{hi}"])
                                S.dve(lambda e, hi=hi: e.tensor_copy(out=Ahi[hi][:, 0:nq], in_=Aac[hi][:, 0:nq]), [f"A{hi}"], [f"Ahi{hi}"])
                                S.dve(lambda e, hi=hi: e.tensor_tensor(out=Alo[hi][:, 0:nq], in0=Aac[hi][:, 0:nq], in1=Ahi[hi][:, 0:nq], op=ALU.subtract), [f"A{hi}", f"Ahi{hi}"], [f"Alo{hi}"])
                                S.act(lambda e, t_=t_, w_=w_, nk=nk: e.activation(out=w_[:nk, 0:nq], in_=t_[:nk, 0:nq], func=AF.Exp), [f"tb{b2}"], [f"wT{b2}"])
                                if mask is not None:
                                    S.pool(lambda e, w_=w_, nk=nk, mask=mask: e.tensor_tensor(out=w_[:nk, 0:nq], in0=w_[:nk, 0:nq], in1=mask, op=ALU.mult), [f"wT{b2}", "cbf"], [f"wT{b2}"])
                                if kind == "s" and os.environ.get("ATT_CUT"):
                                    DBG["cnt"] = DBG.get("cnt", 0) + 1
                                    if DBG["cnt"] >= int(os.environ["ATT_CUT"]):
                                        DBG["info"] = dict(pz=pz, pB=pB, pO=pO, b2=b2, h=h, nk=nk, nq=nq, q0=q0, sl=sl, bi=bi, kb=kb)
                                        raise StopBuild()
                                lastblk = (gi == 0 and bi == 0)
                                S.pe(lambda e, w_=w_, nk=nk, pO=pO, h=h, bi=bi, sl=sl, first=first, lastblk=lastblk: e.matmul(PS[pO][:, 0:nq], lhsT=VS[sl][:nk, bi, h * 128:(h + 1) * 128], rhs=w_[:nk, 0:nq], start=first, stop=lastblk),
                                     [f"wT{b2}", f"VS{sl}"], [PK[pO]])
                            first = False
                    for hi in range(2):
                        h = hp * 2 + hi
                        S.act(lambda e, h=h, hi=hi: e.activation(out=oT[:, h, q0:q0 + nq], in_=PS[4 + hi][:, 0:nq], func=AF.Copy), [PK[4 + hi]], ["oT"])
            for u in units:
                do_unit(*u)
            S.barrier()
            wv, wk = w_next(f"bo{l}")
            for c in range(nch):
                for hb in range(2):
                    pi = c * 2 + hb
                    for h in range(4):
                        S.pe(lambda e, c=c, hb=hb, h=h, wv=wv, pi=pi: e.matmul(PS[pi][:ct, :], lhsT=oT[:, h, c * ct:(c + 1) * ct], rhs=wv[:, h, hb * 512:(hb + 1) * 512], start=(h == 0), stop=(h == 3)),
                             [wk, "oT"], [PK[pi]])
            for c in range(nch):
                layer_norm(ct, c, (c * 2, c * 2 + 1), lng, lnb, "lnp", x_tok[:ct, c, :], "x_tok", vb, "vb", alpha=ALPHA)
            S.barrier()

        def mlp_ple(l, kind, t0, ct, nch, tix):
            TT = ct * nch
            top[0] = PERSIST_TOP
            hTm = alloc(BF16, 32, 512)
            rl = [alloc(F32, 512) for _ in range(2)]
            vb = alloc(F32, D)
            xbf = alloc(BF16, D)
            ptok = alloc(F32, 4, PLE); pbf = alloc(BF16, PLE); pT = alloc(BF16, 2, 512)
            sg = alloc(F32, 512); tg = alloc(F32, 512)
            load_ln(ln2_g[l], ln2_b[l])
            if kind == "p":
                S.dma(ptok[:ct, 0:nch, :], p_p[l][t0:t0 + TT, :].rearrange("(c p) d -> p c d", p=ct), [], ["ptok"], "ptok")
            else:
                S.dma(ptok[:ct, 0:nch, :], p_s[l].rearrange("c p d -> p c d"), [], ["ptok"], "ptok")
            transpose_to_xT(ct, nch, x_tok, "x_tok", xT, "xT", 8, xbf, "xbf")
            for i in range(8):
                wv, wk = w_next(f"w1{l}_{i}")
                for j in range(4):
                    oc = i * 4 + j
                    pi = psr([1, 2, 3, 4])
                    for kc in range(8):
                        S.pe(lambda e, kc=kc, j=j, wv=wv, pi=pi: e.matmul(PS[pi][:, 0:TT], lhsT=wv[:, kc, j * 128:(j + 1) * 128], rhs=xT[:, kc, 0:TT], start=(kc == 0), stop=(kc == 7)), [wk, "xT"], [PK[pi]])
                    b = oc % 2
                    S.act(lambda e, b=b, pi=pi: e.activation(out=rl[b][:, 0:TT], in_=PS[pi][:, 0:TT], func=AF.Relu), [PK[pi]], [f"rl{b}"])
                    eng = S.pool if oc % 2 == 0 else S.dve
                    eng(lambda e, b=b, oc=oc: e.tensor_tensor(out=hTm[:, oc, 0:TT], in0=rl[b][:, 0:TT], in1=rl[b][:, 0:TT], op=ALU.mult), [f"rl{b}"], ["hTm"])
            S.barrier()
            for i in range(8):
                wv, wk = w_next(f"w2{l}_{i}")
                for c in range(nch):
                    for hb in range(2):
                        pi = c * 2 + hb
                        for k in range(4):
                            S.pe(lambda e, c=c, hb=hb, k=k, i=i, wv=wv, pi=pi: e.matmul(PS[pi][:ct, :], lhsT=hTm[:, i * 4 + k, c * ct:(c + 1) * ct], rhs=wv[:, k, hb * 512:(hb + 1) * 512],
                                                                                 start=(i == 0 and k == 0), stop=(i == 7 and k == 3)), [wk, "hTm"], [PK[pi]])
            for c in range(nch):
                layer_norm(ct, c, (c * 2, c * 2 + 1), lng, lnb, "lnp", x_tok[:ct, c, :], "x_tok", vb, "vb", alpha=ALPHA)
            S.barrier()
            transpose_to_xT(ct, nch, x_tok, "x_tok", xT, "xT", 8, xbf, "xbf")
            transpose_to_xT(ct, nch, ptok, "ptok", pT, "pT", 2, pbf, "pbf")
            gws = [w_next(f"g{l}_0"), w_next(f"g{l}_1", back=1)]
            pwv, pwk = w_next(f"pw{l}", back=2)
            for c in range(nch):
                for hb in range(2):
                    wv, wk = gws[hb]
                    pg = psr([1, 2]); pp = psr([3, 4])
                    for kc in range(8):
                        S.pe(lambda e, kc=kc, c=c, wv=wv, pg=pg: e.matmul(PS[pg][:ct, :], lhsT=xT[:, kc, c * ct:(c + 1) * ct], rhs=wv[:, kc, :], start=(kc == 0), stop=(kc == 7)), [wk, "xT"], [PK[pg]])
                    for k in range(2):
                        S.pe(lambda e, k=k, c=c, hb=hb, pp=pp: e.matmul(PS[pp][:ct, :], lhsT=pT[:, k, c * ct:(c + 1) * ct], rhs=pwv[:, k, hb * 512:(hb + 1) * 512], start=(k == 0), stop=(k == 1)), [pwk, "pT"], [PK[pp]])
                    S.act(lambda e, pg=pg: e.activation(out=sg[:ct, :], in_=PS[pg][:ct, :], func=AF.Sigmoid), [PK[pg]], ["sg"])
                    S.dve(lambda e, pp=pp: e.tensor_tensor(out=tg[:ct, :], in0=PS[pp][:ct, :], in1=sg[:ct, :], op=ALU.mult), [PK[pp], "sg"], ["tg"])
                    S.pool(lambda e, c=c, hb=hb: e.tensor_tensor(out=x_tok[:ct, c, hb * 512:(hb + 1) * 512], in0=x_tok[:ct, c, hb * 512:(hb + 1) * 512], in1=tg[:ct, :], op=ALU.add), ["x_tok", "tg"], ["x_tok"])
            S.barrier()

        def kv_stage(kind, t0, ct, nch, tix):
            TT = ct * nch
            top[0] = PERSIST_TOP
            kvn = alloc(F32, 4, D); vb = alloc(F32, D)
            kvbf = alloc(BF16, D); kvT = alloc(BF16, 8, 512)
            kg = alloc(F32, D); kb_ = alloc(F32, D)
            ktk = alloc(F32, 4, SBW); vtk = alloc(F32, 4, SBW); vbf = alloc(BF16, 4, SBW)
            kTt = alloc(BF16, 4, 512)
            S.dma(kg, kv_g.partition_broadcast(128), [], ["kgb"], "kgb")
            S.dma(kb_, kv_b.partition_broadcast(128), [], ["kgb"], "kgb")
            for c in range(nch):
                layer_norm(ct, c, None, kg, kb_, "kgb", kvn[:ct, c, :], "kvn", vb, "vb", src=x_tok[:ct, c, :], src_key="x_tok")
            KVC = int(os.environ.get("KV_CUT", "9"))
            if KVC <= 0:
                S.barrier(); return
            transpose_to_xT(ct, nch, kvn, "kvn", kvT, "kvT", 8, kvbf, "kvbf")
            wks = [w_next("kv_0"), w_next("kv_1", back=1)]
            if KVC <= 1:
                S.barrier(); return
            for c in range(nch):
                for hb in range(2):
                    wv, wk = wks[hb]
                    pi = psr([1, 2])
                    for kc in range(8):
                        S.pe(lambda e, kc=kc, c=c, wv=wv, pi=pi: e.matmul(PS[pi][:ct, :], lhsT=kvT[:, kc, c * ct:(c + 1) * ct], rhs=wv[:, kc, :], start=(kc == 0), stop=(kc == 7)), [wk, "kvT"], [PK[pi]])
                    dst = ktk if hb == 0 else vtk
                    dk = "ktk" if hb == 0 else "vtk"
                    S.act(lambda e, dst=dst, c=c, pi=pi: e.activation(out=dst[:ct, c, :], in_=PS[pi][:ct, :], func=AF.Copy), [PK[pi]], [dk])
                    if hb == 1:
                        S.dve(lambda e, c=c, pi=pi: e.tensor_copy(out=vbf[:ct, c, :], in_=PS[pi][:ct, :]), [PK[pi]], ["vbf"])
            if KVC <= 2:
                S.barrier(); return
            wv, wk = wks[0]
            for h in range(4):
                pi = psr([3, 4])
                for kc in range(8):
                    S.pe(lambda e, kc=kc, h=h, wv=wv, pi=pi: e.matmul(PS[pi][:, 0:TT], lhsT=wv[:, kc, h * 128:(h + 1) * 128], rhs=kvT[:, kc, 0:TT], start=(kc == 0), stop=(kc == 7)), [wk, "kvT"], [PK[pi]])
                S.act(lambda e, h=h, pi=pi: e.activation(out=kTt[:, h, 0:TT], in_=PS[pi][:, 0:TT], func=AF.Copy), [PK[pi]], ["kTt"])
            if KVC <= 3:
                S.barrier(); return
            if kind == "p":
                S.dma(k_p[t0:t0 + TT, :].rearrange("(c p) d -> p c d", p=ct), ktk[:ct, 0:nch, :], ["ktk"], ["k_out"], "ktk", q="pool")
                S.dma(v_p[t0:t0 + TT, :].rearrange("(c p) d -> p c d", p=ct), vtk[:ct, 0:nch, :], ["vtk"], ["v_out"], "vtk", q="pool")
                S.dma(vv_p[t0:t0 + TT, :].rearrange("(c p) d -> p c d", p=ct), vbf[:ct, 0:nch, :], ["vbf"], ["vv_p"], "vbf", q="pool")
                S.dma(kT_p[:, :, t0:t0 + TT], kTt[:, :, 0:TT], ["kTt"], ["kT_p"], "kTt", q="pool")
            else:
                S.dma(k_s.rearrange("c p d -> p c d"), ktk[:ct, 0:nch, :], ["ktk"], ["k_out"], "ktk", q="pool")
                S.dma(v_s.rearrange("c p d -> p c d"), vtk[:ct, 0:nch, :], ["vtk"], ["v_out"], "vtk", q="pool")
                for s_ in range(nch):
                    S.dma(vv_s[s_][PAST:PAST + ct, :], vbf[:ct, s_, :], ["vbf"], [f"vv_s{s_}"], f"vbf{s_}", q="pool")
                    S.dma(kT_s[s_][:, :, PAST:PAST + ct], kTt[:, :, s_ * ct:(s_ + 1) * ct], ["kTt"], [f"kT_s{s_}"], f"kTt{s_}", q="pool")
            S.barrier()

        try:
            stage("prologue")
            for tix, (kind, t0, ct, nch) in enumerate(tiles):
                TT = ct * nch
                if kind == "p":
                    S.dma(x_tok[:ct, 0:nch, :], x_p[t0:t0 + TT, :].rearrange("(c p) d -> p c d", p=ct), [], ["x_tok"], "x_tok")
                else:
                    S.dma(x_tok[:ct, 0:nch, :], x_s.rearrange("c p d -> p c d"), [], ["x_tok"], "x_tok")
                for l in range(DEPTH):
                    top[0] = PERSIST_TOP
                    xbf0 = alloc(BF16, D)
                    transpose_to_xT(ct, nch, x_tok, "x_tok", xT, "xT", 8, xbf0, "xbf0")
                    S.barrier()
                    stage(f"xT{l}")
                    if l < 2:
                        mamba_mixer(l, kind, t0, ct, nch, tix)
                    else:
                        attn_mixer(l, kind, t0, ct, nch, tix)
                    stage(f"mixer{l}")
                    mlp_ple(l, kind, t0, ct, nch, tix)
                    stage(f"mlp{l}")
                    if l == 1:
                        kv_stage(kind, t0, ct, nch, tix)
                        stage("kv")
                if kind == "p":
                    S.dma(y_p[t0:t0 + TT, :].rearrange("(c p) d -> p c d", p=ct), x_tok[:ct, 0:nch, :], ["x_tok"], ["y_out"], "x_tok", q="pool")
                else:
                    S.dma(y_s.rearrange("c p d -> p c d"), x_tok[:ct, 0:nch, :], ["x_tok"], ["y_out"], "x_tok", q="pool")
            assert wstate["next_use"] == len(worder)
        except StopBuild:
            if kind == "p":
                S.dma(y_p[t0:t0 + TT, :].rearrange("(c p) d -> p c d", p=ct), x_tok[:ct, 0:nch, :], ["x_tok"], ["y_out"], "x_tok", q="pool")
            else:
                S.dma(y_s.rearrange("c p d -> p c d"), x_tok[:ct, 0:nch, :], ["x_tok"], ["y_out"], "x_tok", q="pool")
        S.finish()
    return nc, S


def make_consts():
    a = np.arange(128)
    ident = (a[:, None] == a[None, :]).astype(np.float32)
    GT = (a[:, None] > a[None, :]).astype(np.float32)
    LE = (a[:, None] <= a[None, :]).astype(np.float32)
    LT = (a[:, None] < a[None, :]).astype(np.float32)
    ones = np.ones((128, 128), np.float32)
    M4 = np.zeros((128, 4, 512), np.float32)
    for kl in range(4):
        M4[:, kl, kl * 128:(kl + 1) * 128] = LT
        M4[:, kl, (kl + 1) * 128:] = 1.0
    cbf = np.concatenate([ident, GT, LE, ones, M4.reshape(128, 2048)], axis=1).astype(ml_dtypes.bfloat16)
    cf = np.concatenate([ident, LE, ones], axis=1).astype(np.float32)
    return cbf, cf


_CACHE = {}


def run(inputs, SEQ, n_cores=8, KSTOP=None):
    if SEQ not in _CACHE:
        _CACHE[SEQ] = build(SEQ, KSTOP=KSTOP)
    nc, S = _CACHE[SEQ]
    cbf, cf = make_consts()
    f = lambda a: np.ascontiguousarray(np.asarray(a, dtype=np.float32))
    wnames = ["a_w_in", "a_conv_w", "a_conv_b", "a_dt_bias", "a_A_log", "a_d_skip", "a_norm_g", "a_w_out", "kv_norm_g", "kv_norm_b",
              "w_kv", "b_w_q", "b_w_out", "ln1_g", "ln1_b", "ln2_g", "ln2_b", "mlp_w1", "mlp_w2", "ple_w", "ple_gate_w"]
    shared = {n: f(inputs[n]) for n in wnames}
    shared["cst_bf"] = cbf; shared["cst_f"] = cf
    in_maps = []
    for b in range(n_cores):
        m = dict(shared)
        m["x_p"] = f(inputs["x_prompt"][b]); m["x_s"] = f(inputs["x_sample"][2 * b:2 * b + 2])
        m["p_p"] = f(inputs["p_prompt"][:, b]); m["p_s"] = f(inputs["p_sample"][:, 2 * b:2 * b + 2])
        m["st_ssm"] = f(inputs["state_ssm"][:, 2 * b:2 * b + 2]).reshape(2, 2, NH * HP, NS)
        m["st_conv"] = f(inputs["state_conv"][:, 2 * b:2 * b + 2])
        m["c_k"] = f(inputs["cache_k"][2 * b:2 * b + 2]).reshape(2, PAST, SBW)
        m["c_v"] = f(inputs["cache_v"][2 * b:2 * b + 2]).reshape(2, PAST, SBW)
        in_maps.append(m)
    res = run_bass_kernel_spmd(nc, in_maps, core_ids=list(range(n_cores)))
    R = res.results
    B = n_cores
    y_prompt = np.stack([R[b]["y_p"] for b in range(B)])
    y_sample = np.concatenate([R[b]["y_s"] for b in range(B)], axis=0)
    ssm_p = np.stack([R[b]["ssm_p"].reshape(2, NH, HP, NS) for b in range(B)], axis=1)
    conv_p = np.stack([R[b]["conv_p"] for b in range(B)], axis=1)
    k_p = np.stack([R[b]["k_p"].reshape(SEQ, 4, 128) for b in range(B)])
    v_p = np.stack([R[b]["v_p"].reshape(SEQ, 4, 128) for b in range(B)])
    ssm_s = np.concatenate([R[b]["ssm_s"].reshape(2, 2, NH, HP, NS) for b in range(B)], axis=1)
    conv_s = np.concatenate([R[b]["conv_s"] for b in range(B)], axis=1)
    k_s = np.concatenate([R[b]["k_s"].reshape(2, 16, 4, 128) for b in range(B)], axis=0)
    v_s = np.concatenate([R[b]["v_s"].reshape(2, 16, 4, 128) for b in range(B)], axis=0)
    outs = (y_prompt, y_sample, ssm_p, conv_p, k_p, v_p, ssm_s, conv_s, k_s, v_s)
    return tuple(np.ascontiguousarray(o.astype(np.float32)) for o in outs)


def kernel(**inputs):
    return run(inputs, int(np.asarray(inputs["x_prompt"]).shape[1]))
```
